# Optimizing a Trainium2 kernel written in Bass

```python
import jax, jax.numpy as jnp
from jax import lax
import numpy as np

D_MODEL = 1024
BATCH = 4
SEQ = 8192
DEPTH = 1

N_META = 16
D_MIX = D_MODEL
MLA_HEADS = 8
QK_NOPE_DIM = 64
QK_ROPE_DIM = 32
QK_HEAD_DIM = QK_NOPE_DIM + QK_ROPE_DIM
V_HEAD_DIM = 64
Q_LORA_RANK = 256
KV_LORA_RANK = 128
MLA_WIDTH = MLA_HEADS * V_HEAD_DIM
FOURIER_GROUPS = 4
FOURIER_WIDTH = D_MIX - MLA_WIDTH
FOURIER_GROUP_DIM = FOURIER_WIDTH // FOURIER_GROUPS
IN_WIDTH = Q_LORA_RANK + KV_LORA_RANK + QK_ROPE_DIM + FOURIER_WIDTH
D_FF = ((8 * D_MODEL // 3 + 255) // 256) * 256
ROPE_THETA = 10000.0
Q_BLOCK = 128
EPS = 1e-6

kernel_name = "hymba_mla_fnet_sandwich_encoder"


def rmsnorm(x, g):
    xf = x.astype(jnp.float32)
    y = xf * lax.rsqrt(jnp.mean(xf * xf, axis=-1, keepdims=True) + EPS)
    return (y * g.astype(jnp.float32)).astype(x.dtype)


def rope_tables(length, dim):
    pos = jnp.arange(length, dtype=jnp.float32)
    inv_freq = ROPE_THETA ** (-jnp.arange(0, dim, 2, dtype=jnp.float32) / dim)
    ang = pos[:, None] * inv_freq[None, :]
    return jnp.cos(ang), jnp.sin(ang)


def apply_rope(x, cos, sin):
    half = x.shape[-1] // 2
    x1 = x[..., :half].astype(jnp.float32)
    x2 = x[..., half:].astype(jnp.float32)
    c = cos[None, :, None, :]
    s = sin[None, :, None, :]
    out = jnp.concatenate([x1 * c - x2 * s, x2 * c + x1 * s], axis=-1)
    return out.astype(x.dtype)


def mla_mixer(cq, ckv, k_rope_raw, q_a_norm, w_q_b, kv_a_norm, w_kv_b):
    B, L, _ = cq.shape
    cos, sin = rope_tables(L, QK_ROPE_DIM)
    cq = rmsnorm(cq, q_a_norm)
    q = (cq @ w_q_b).reshape(B, L, MLA_HEADS, QK_HEAD_DIM)
    q = jnp.concatenate([q[..., :QK_NOPE_DIM], apply_rope(q[..., QK_NOPE_DIM:], cos, sin)], axis=-1)
    ckv = rmsnorm(ckv, kv_a_norm)
    kv = (ckv @ w_kv_b).reshape(B, L, MLA_HEADS, QK_NOPE_DIM + V_HEAD_DIM)
    k_nope, v = kv[..., :QK_NOPE_DIM], kv[..., QK_NOPE_DIM:]
    k_rope = apply_rope(k_rope_raw[:, :, None, :], cos, sin)
    k = jnp.concatenate([k_nope, jnp.broadcast_to(k_rope, (B, L, MLA_HEADS, QK_ROPE_DIM))], axis=-1)
    scale = 1.0 / float(np.sqrt(QK_HEAD_DIM))
    n_blk = (L + Q_BLOCK - 1) // Q_BLOCK
    Lp = n_blk * Q_BLOCK
    qp = jnp.pad(q, ((0, 0), (0, Lp - L), (0, 0), (0, 0)))
    qb = qp.reshape(B, n_blk, Q_BLOCK, MLA_HEADS, QK_HEAD_DIM).transpose(1, 0, 2, 3, 4)

    def attend(q_blk):
        s = jnp.einsum('bqhd,bkhd->bhqk', q_blk, k).astype(jnp.float32) * scale
        p = jax.nn.softmax(s, axis=-1).astype(v.dtype)
        return jnp.einsum('bhqk,bkhd->bqhd', p, v)

    o = lax.map(attend, qb)
    o = o.transpose(1, 0, 2, 3, 4).reshape(B, Lp, MLA_WIDTH)[:, :L]
    return o


def fourier_mixer(f_in, w_fourier, b_fourier):
    B, L, _ = f_in.shape
    f = f_in.reshape(B, L, FOURIER_GROUPS, FOURIER_GROUP_DIM).astype(jnp.float32)
    F = jnp.fft.fftn(f, axes=(1, 3), norm='ortho').real.astype(f_in.dtype)
    y = jnp.einsum('blgc,gcd->blgd', F, w_fourier) + b_fourier
    return y.reshape(B, L, FOURIER_WIDTH)


def setup_inputs(seed: int = 0) -> dict:
    key = jax.random.key(seed)
    ks = jax.random.split(key, 20)
    f32 = jnp.float32

    def w(k, shape, fan_in):
        return jax.random.normal(k, shape, f32) * fan_in ** -0.5

    def gain(k, shape):
        return 1.0 + 0.01 * jax.random.normal(k, shape, f32)

    return {
        "x": jax.random.normal(ks[0], (BATCH, SEQ, D_MODEL), f32),
        "meta_tokens": jax.random.normal(ks[1], (N_META, D_MODEL), f32),
        "norm_pre_mix": gain(ks[2], (DEPTH, D_MODEL)),
        "w_in": w(ks[3], (DEPTH, D_MODEL, IN_WIDTH), D_MODEL),
        "q_a_norm": gain(ks[4], (DEPTH, Q_LORA_RANK)),
        "w_q_b": w(ks[5], (DEPTH, Q_LORA_RANK, MLA_HEADS * QK_HEAD_DIM), Q_LORA_RANK),
        "kv_a_norm": gain(ks[6], (DEPTH, KV_LORA_RANK)),
        "w_kv_b": w(ks[7], (DEPTH, KV_LORA_RANK, MLA_HEADS * (QK_NOPE_DIM + V_HEAD_DIM)), KV_LORA_RANK),
        "w_fourier": w(ks[8], (DEPTH, FOURIER_GROUPS, FOURIER_GROUP_DIM, FOURIER_GROUP_DIM), FOURIER_GROUP_DIM),
        "b_fourier": 0.01 * jax.random.normal(ks[9], (DEPTH, FOURIER_GROUPS, FOURIER_GROUP_DIM), f32),
        "mix_gain_attn": gain(ks[10], (DEPTH, MLA_WIDTH)),
        "mix_gain_fourier": gain(ks[11], (DEPTH, FOURIER_WIDTH)),
        "w_o": w(ks[12], (DEPTH, D_MIX, D_MODEL), D_MIX),
        "norm_post_mix": gain(ks[13], (DEPTH, D_MODEL)),
        "norm_pre_ffn": gain(ks[14], (DEPTH, D_MODEL)),
        "w_gate": w(ks[15], (DEPTH, D_MODEL, D_FF), D_MODEL),
        "w_up": w(ks[16], (DEPTH, D_MODEL, D_FF), D_MODEL),
        "w_down": w(ks[17], (DEPTH, D_FF, D_MODEL), D_FF),
        "norm_post_ffn": gain(ks[18], (DEPTH, D_MODEL)),
    }


def reference(x, meta_tokens, norm_pre_mix, w_in, q_a_norm, w_q_b, kv_a_norm, w_kv_b,
              w_fourier, b_fourier, mix_gain_attn, mix_gain_fourier, w_o, norm_post_mix,
              norm_pre_ffn, w_gate, w_up, w_down, norm_post_ffn):
    B = x.shape[0]
    meta = jnp.broadcast_to(meta_tokens[None].astype(x.dtype), (B, N_META, D_MODEL))
    h_res = jnp.concatenate([meta, x], axis=1)
    o1 = Q_LORA_RANK
    o2 = o1 + KV_LORA_RANK
    o3 = o2 + QK_ROPE_DIM
    for i in range(DEPTH):
        h = rmsnorm(h_res, norm_pre_mix[i])
        z = h @ w_in[i]
        attn = mla_mixer(z[..., :o1], z[..., o1:o2], z[..., o2:o3],
                         q_a_norm[i], w_q_b[i], kv_a_norm[i], w_kv_b[i])
        four = fourier_mixer(z[..., o3:], w_fourier[i], b_fourier[i])
        merged = jnp.concatenate([rmsnorm(attn, mix_gain_attn[i]),
                                  rmsnorm(four, mix_gain_fourier[i])], axis=-1)
        h_res = h_res + rmsnorm(merged @ w_o[i], norm_post_mix[i])
        h = rmsnorm(h_res, norm_pre_ffn[i])
        ff = (jax.nn.silu(h @ w_gate[i]) * (h @ w_up[i])) @ w_down[i]
        h_res = h_res + rmsnorm(ff, norm_post_ffn[i])
    return h_res[:, N_META:]
```

```python
import numpy as np
import ml_dtypes
import concourse.bass as bass
import concourse.mybir as mybir
from concourse.bass_utils import run_bass_kernel_spmd

F32 = mybir.dt.float32
BF16 = mybir.dt.bfloat16
AF = mybir.ActivationFunctionType
ALU = mybir.AluOpType

D = 1024
SEQ = 8192
NMETA = 16
L = SEQ + NMETA
NOWN = 4096
NQB = 8
NTB = 17
NLB = 65
H = 8
DFF = 2816
NJ = DFF // 128
EPS = 1e-6
SCALE = 1.0 / float(np.sqrt(96.0))
NCHK = 9

ENGS = ["pe", "act", "dve", "pool", "sp"]


class Prog:
    def __init__(self):
        self.ops = {e: [] for e in ENGS}
        self.last_w = {}
        self.readers = {}
        self.dma_cnt = {}
        self.pending = {e: set() for e in ENGS}

    def op(self, eng, fn, reads=(), writes=(), dma=None):
        idx = len(self.ops[eng])
        deps = set(self.pending[eng])
        self.pending[eng] = set()
        for r in reads:
            w = self.last_w.get(r)
            if w is not None:
                deps.add(w)
        for w_ in writes:
            w = self.last_w.get(w_)
            if w is not None:
                deps.add(w)
            for d in self.readers.get(w_, {}).values():
                deps.add(d)
        if dma is not None:
            cnt = self.dma_cnt.get(dma, 0) + 1
            self.dma_cnt[dma] = cnt
            tok = ("dma", dma, cnt)
            rkey = ("dma", dma)
        else:
            tok = ("eng", eng, idx)
            rkey = ("eng", eng)
        deps.discard(tok)
        self.ops[eng].append(dict(fn=fn, deps=deps, tok=tok, dma=dma))
        for r in reads:
            self.readers.setdefault(r, {})[rkey] = tok
        for w_ in writes:
            self.last_w[w_] = tok
            self.readers[w_] = {}
        return tok

    def barrier(self):
        alld = set()
        for e in ENGS:
            for i in range(len(self.ops[e]) - 1, -1, -1):
                if self.ops[e][i]["dma"] is None:
                    alld.add(self.ops[e][i]["tok"])
                    break
        for k, c in self.dma_cnt.items():
            alld.add(("dma", k, c))
        for e in ENGS:
            self.pending[e] |= alld

    def emit(self, nc, block_engines):
        pub = {e: set() for e in ENGS}
        for e in ENGS:
            for o in self.ops[e]:
                for d in o["deps"]:
                    if d[0] == "eng":
                        if d[1] == e and e == "pe":
                            continue
                        pub[d[1]].add(d[2])
        pubcount = {}
        for e in ENGS:
            c = 0
            m = {}
            for i, o in enumerate(self.ops[e]):
                if o["dma"] is None and i in pub[e]:
                    c += 1
                    m[i] = c
            pubcount[e] = m
        esem = {e: nc.alloc_semaphore("sem_" + e) for e in ENGS}
        dsem = {k: nc.alloc_semaphore("dsem_%d" % i) for i, k in enumerate(self.dma_cnt)}
        final_w = {k: self.dma_cnt.get(k, 0) for k in ("w_sp", "w_pool")}

        def run(e, eng):
            waited = {}
            for i, o in enumerate(self.ops[e]):
                need = {}
                for d in o["deps"]:
                    if d[0] == "eng":
                        if d[1] == e and e == "pe":
                            continue
                        key = ("e", d[1])
                        val = pubcount[d[1]][d[2]]
                    else:
                        key = ("d", d[1])
                        val = 16 * (final_w[d[1]] if d[1] in final_w else d[2])
                    if val > need.get(key, 0):
                        need[key] = val
                for key, val in need.items():
                    if waited.get(key, 0) >= val:
                        continue
                    waited[key] = val
                    sem = esem[key[1]] if key[0] == "e" else dsem[key[1]]
                    eng.wait_ge(sem, val)
                ins = o["fn"](eng)
                if o["dma"] is not None:
                    ins.then_inc(dsem[o["dma"]], 16)
                elif i in pub[e]:
                    ins.then_inc(esem[e], 1)

        for e in ENGS:
            block_engines[e](lambda eng, e=e: run(e, eng))


class Arena:
    def __init__(self, ar, total):
        self.ar = ar
        self.free = [(0, total)]
        self.allocs = {}

    def alloc(self, name, nbytes):
        nbytes = (nbytes + 63) // 64 * 64
        for i, (o, s) in enumerate(self.free):
            if s >= nbytes:
                self.free[i] = (o + nbytes, s - nbytes)
                self.allocs[name] = (o, nbytes)
                return o
        raise MemoryError("SBUF arena full allocating %s (%d B); free=%s" % (name, nbytes, self.free))

    def release(self, *names):
        for name in names:
            o, s = self.allocs.pop(name)
            self.free.append((o, s))
        self.free.sort()
        merged = []
        for o, s in self.free:
            if s == 0:
                continue
            if merged and merged[-1][0] + merged[-1][1] == o:
                merged[-1] = (merged[-1][0], merged[-1][1] + s)
            else:
                merged.append((o, s))
        self.free = merged

    def tile(self, name, shape, dtype):
        esz = 4 if dtype == F32 else 2
        n = 1
        for s in shape[1:]:
            n *= s
        off = self.alloc(name, n * esz)
        v = self.ar[:, off // 2: off // 2 + n * esz // 2]
        if dtype == F32:
            v = v.bitcast(F32)
        if len(shape) > 2:
            names = " ".join("a%d" % i for i in range(len(shape) - 1))
            kw = {"a%d" % i: shape[i + 1] for i in range(len(shape) - 1)}
            v = v.rearrange("p (%s) -> p %s" % (names, names), **kw)
        if shape[0] < 128:
            v = v[0:shape[0]]
        return v


def build_program(stop_after=None):
    nc = bass.Bass("TRN2", target_bir_lowering=False)
    P = Prog()
    _stop = [False]

    def din(name, shape, dt=F32):
        return nc.dram_tensor(name, list(shape), dt, kind="ExternalInput").ap()

    xperm = din("xperm", [L, D])
    w_in = din("w_in", [D, 928])
    w_q_b = din("w_q_b", [256, 768])
    w_kv_b = din("w_kv_b", [128, 1024])
    w_f = din("w_fourier", [4, 128, 128])
    w_o = din("w_o", [D, D])
    w_gate = din("w_gate", [D, DFF])
    w_up = din("w_up", [D, DFF])
    w_down = din("w_down", [DFF, D])
    d_gpre = din("g_pre", [128, 8])
    d_gq = din("g_q", [128, 2])
    d_gkv = din("g_kv", [128, 1])
    d_ga = din("g_a", [128, 4])
    d_gf = din("g_f", [128, 4])
    d_bf = din("b_f", [128, 4])
    d_gpost = din("g_post", [D])
    d_gpreffn = din("g_preffn", [D])
    d_gpostffn = din("g_postffn", [D])
    d_ktab = din("ktab", [64, L])
    d_qtab = din("qtab", [128, NOWN])
    d_cc = din("cc", [128, 128])
    d_sc = din("sc", [128, 128])
    d_ident = din("ident", [128, 128])
    d_sel = din("sel", [32, 96])
    d_dft = din("dft", [NQB, 128, NCHK, 8, 512], BF16)
    y = nc.dram_tensor("y", [NOWN, D], F32, kind="ExternalOutput").ap()
    scr_g = nc.dram_tensor("scr_g", [NJ, 128, 1024], BF16).ap()
    scr_u = nc.dram_tensor("scr_u", [NJ, 128, 1024], BF16).ap()
    scr_d = nc.dram_tensor("scr_d", [NJ, 128, 1024], BF16).ap()
    scr_f = nc.dram_tensor("scr_f", [NQB, 128, 4, 512], BF16).ap()

    TOTAL = 211968
    AR = nc.alloc_sbuf_tensor("arena", [128, TOTAL // 2], BF16)
    A = Arena(AR, TOTAL)
    PS = nc.alloc_psum_tensor("ps", [128, 4096], F32)

    def bank(b, n=1):
        return PS[:, b * 512:(b + n) * 512]

    def bank_bf(b):
        return PS[:, b * 512:(b + 1) * 512].bitcast(BF16)

    def pb(b):
        return ("ps", b)

    ident = A.tile("ident", [128, 128], BF16)
    ones = A.tile("ones", [128, 128], BF16)
    sel = A.tile("sel", [32, 96], BF16)
    epst = A.tile("eps", [128, 1], F32)
    gq = A.tile("gq", [128, 2], F32)
    gkv = A.tile("gkv", [128, 1], F32)
    ga = A.tile("ga", [128, 4], F32)
    gf = A.tile("gf", [128, 4], F32)
    bfb = A.tile("bfb", [128, 4], F32)
    gpre = A.tile("gpre", [128, 8], F32)
    Wq = A.tile("Wq", [128, 2, 8, 128], BF16)
    Wkvb = A.tile("Wkvb", [128, 1024], BF16)
    M12 = A.tile("M12", [128, 8, 128], BF16)

    def dma(q, out, in_, key, reads=(), writes=()):
        if key == "w":
            key = "w_" + q
        return P.op(q, lambda e, out=out, in_=in_: e.dma_start(out=out, in_=in_), reads=reads, writes=writes, dma=key)

    P.op("dve", lambda e: e.memset(ones, 1.0), writes=["ones"])
    P.op("dve", lambda e: e.memset(epst, EPS), writes=["eps"])
    for dst, src, nm in [(gq, d_gq, "gq"), (gkv, d_gkv, "gkv"), (ga, d_ga, "ga"), (gf, d_gf, "gf"),
                         (bfb, d_bf, "bfb"), (gpre, d_gpre, "gpre")]:
        dma("sp", dst, src, "w", writes=[nm])
    dma("pool", ident, d_ident, "w", writes=["ident"])
    dma("pool", sel, d_sel, "w", writes=["sel"])

    Win = A.tile("Win", [128, 8, 928], BF16)
    Wkr = A.tile("Wkr", [128, 8, 64], BF16)
    wst = [A.tile("wst%d" % i, [128, 928], F32) for i in range(2)]
    w_in_r = w_in.rearrange("(c p) n -> p c n", p=128)
    for c in range(8):
        s = c % 2
        dma("sp", wst[s], w_in_r[:, c, :], ("wst", s), writes=[("wst", s)])
        P.op("dve", lambda e, c=c, s=s: e.tensor_scalar(out=Win[:, c, :], in0=wst[s], scalar1=gpre[:, c:c + 1],
                                                       scalar2=None, op0=ALU.mult),
             reads=[("wst", s), "gpre"], writes=["Win"])
        P.op("dve", lambda e, c=c, s=s: e.tensor_scalar(out=Wkr[:, c, 0:32], in0=wst[s][:, 384:416],
                                                       scalar1=gpre[:, c:c + 1], scalar2=None, op0=ALU.mult),
             reads=[("wst", s), "gpre"], writes=["Wkr"])
        P.op("dve", lambda e, c=c, s=s: e.tensor_scalar(out=Wkr[:, c, 32:48], in0=wst[s][:, 400:416],
                                                       scalar1=gpre[:, c:c + 1], scalar2=None, op0=ALU.mult),
             reads=[("wst", s), "gpre"], writes=["Wkr"])
        P.op("dve", lambda e, c=c, s=s: e.tensor_scalar(out=Wkr[:, c, 48:64], in0=wst[s][:, 384:400],
                                                       scalar1=gpre[:, c:c + 1], scalar2=None, op0=ALU.mult),
             reads=[("wst", s), "gpre"], writes=["Wkr"])
    wq_r = w_q_b.rearrange("(c p) (h e) -> p c h e", p=128, e=96)
    for c in range(2):
        dma("pool", Wq[:, c, :, 0:96], wq_r[:, c, :, :], "w", writes=[("Wq", c, 0)])
        dma("pool", Wq[:, c, :, 96:112], wq_r[:, c, :, 80:96], "w", writes=[("Wq", c, 1)])
        dma("pool", Wq[:, c, :, 112:128], wq_r[:, c, :, 64:80], "w", writes=[("Wq", c, 2)])
    dma("pool", Wkvb, w_kv_b, "w", writes=["Wkvb"])
    ccs = A.tile("ccs", [128, 2, 128], BF16)
    wfs = A.tile("wfs", [128, 4, 128], BF16)
    dma("pool", ccs[:, 0, :], d_cc, "w", writes=[("ccs", 0)])
    dma("pool", ccs[:, 1, :], d_sc, "w", writes=[("ccs", 1)])
    dma("pool", wfs, w_f.rearrange("g c d -> c g d"), "w", writes=["wfs"])
    for m in range(2):
        for g in range(4):
            P.op("pe", lambda e, m=m, g=g: e.matmul(bank(m)[:, g * 128:(g + 1) * 128], ccs[:, m, :], wfs[:, g, :],
                                                    start=True, stop=True),
                 reads=[("ccs", m), "wfs"], writes=[pb(m)])
        P.op("dve", lambda e, m=m: e.tensor_copy(out=M12[:, m * 4:(m + 1) * 4, :],
                                                 in_=bank(m).rearrange("p (g d) -> p g d", d=128)),
             reads=[pb(m)], writes=["M12"])
    if stop_after == 'W':
        return _finish(nc, P)
    ckvT = A.tile("ckvT", [128, L], BF16)
    kropeT = A.tile("kropeT", [32, L], BF16)
    cqT = A.tile("cqT", [128, 2, NOWN], BF16)
    f_sb = A.tile("f_sb", [128, NLB, 512], BF16)
    xt = [A.tile("xt%d" % i, [128, D], F32) for i in range(3)]
    hb = [A.tile("hb%d" % i, [128, D], BF16) for i in range(4)]
    hT = [A.tile("hT%d" % i, [128, 8, 512], BF16) for i in range(2)]
    junk = A.tile("junk", [128, D], BF16)
    ssq = A.tile("ssq", [128, 8], F32)
    sq = A.tile("sq", [128, 3, 512], BF16)
    lnb = A.tile("lnb", [128, 2, 512], F32)
    rbc = A.tile("rbc", [128, 2, 512], F32)
    ktb = [A.tile("ktb%d" % i, [64, 512], F32) for i in range(2)]
    krp = A.tile("krp", [64, 512], F32)
    krp2 = A.tile("krp2", [32, 512], F32)
    ftmp = [A.tile("ftmp%d" % i, [128, 512], BF16) for i in range(2)]

    def rstd_col(col, n, nt):
        P.op("act", lambda e: e.activation(out=ssq[:nt, col:col + 1], in_=ssq[:nt, col:col + 1], func=AF.Ln,
                                           scale=1.0 / n, bias=epst[:nt, :]),
             reads=[("ssq", col), "eps"], writes=[("ssq", col)])
        P.op("act", lambda e: e.activation(out=ssq[:nt, col:col + 1], in_=ssq[:nt, col:col + 1], func=AF.Exp,
                                           scale=-0.5),
             reads=[("ssq", col)], writes=[("ssq", col)])

    ptiles = []
    for tb in range(NTB):
        ntok_ = 512 if tb < 16 else 16
        for t in range((ntok_ + 127) // 128):
            ptiles.append((tb, t, min(128, ntok_ - t * 128)))
    NHB = 4

    def p_t1(i):
        tb, t, nt = ptiles[i]
        xs = i % 3
        bs = i % NHB
        col = i % 4
        r0 = tb * 512 + t * 128
        dma("sp", xt[xs][:nt, :], xperm[r0:r0 + nt, :], ("xt", xs), writes=[("xt", xs)])
        P.op("act", lambda e: e.activation(out=junk[:nt, :], in_=xt[xs][:nt, :], func=AF.Square,
                                           accum_out=ssq[:nt, col:col + 1]),
             reads=[("xt", xs)], writes=["junk", ("ssq", col)])
        rstd_col(col, D, nt)
        P.op("act", lambda e: e.activation(out=hb[bs][:nt, :], in_=xt[xs][:nt, :], func=AF.Identity,
                                           scale=ssq[:nt, col:col + 1]),
             reads=[("xt", xs), ("ssq", col)], writes=[("hb", bs)])

    def p_t2(i):
        tb, t, nt = ptiles[i]
        bs = i % NHB
        hs = tb % 2
        tbk = i % 2
        for c in range(8):
            P.op("pe", lambda e, c=c: e.transpose(
                bank_bf(tbk)[:, c * 128:c * 128 + nt], hb[bs][:nt, c * 128:(c + 1) * 128], ident[:nt, :nt]),
                 reads=[("hb", bs), "ident"], writes=[pb(tbk)])
        P.op("dve", lambda e: e.tensor_copy(
            out=hT[hs][:, :, t * 128:t * 128 + nt],
            in_=bank_bf(tbk).rearrange("p (c n) -> p c n", n=128)[:, :, :nt]),
             reads=[pb(tbk)], writes=[("hT", hs)])

    def p_proj(tb):
        t0 = tb * 512
        ntok = 512 if tb < 16 else 16
        ntile = (ntok + 127) // 128
        hs = tb % 2
        own = tb < NQB
        ks = tb % 2
        if tb + 1 < NTB:
            ntk = 512 if tb + 1 < 16 else 16
            dma("sp", ktb[1 - ks][:, :ntk], d_ktab[:, (tb + 1) * 512:(tb + 1) * 512 + ntk], ("ktb", 1 - ks),
                writes=[("ktb", 1 - ks)])
        for c in range(8):
            P.op("pe", lambda e, c=c, hs=hs, ntok=ntok: e.matmul(bank(2)[:, :ntok], Win[:, c, 256:384],
                                                                hT[hs][:, c, :ntok], start=(c == 0), stop=(c == 7)),
                 reads=["Win", ("hT", hs)], writes=[pb(2)])
        if own:
            for m in range(2):
                for c in range(8):
                    P.op("pe", lambda e, c=c, m=m, hs=hs: e.matmul(bank(6 + m), Win[:, c, m * 128:(m + 1) * 128],
                                                                  hT[hs][:, c, :], start=(c == 0), stop=(c == 7)),
                         reads=["Win", ("hT", hs)], writes=[pb(6 + m)])
        P.op("act", lambda e, ntok=ntok: e.activation(out=sq[:, 0, :ntok], in_=bank(2)[:, :ntok], func=AF.Square),
             reads=[pb(2)], writes=[("sq", 0)])
        if own:
            for m in range(2):
                P.op("act", lambda e, m=m: e.activation(out=sq[:, 1 + m, :], in_=bank(6 + m), func=AF.Square),
                     reads=[pb(6 + m)], writes=[("sq", 1 + m)])
        for c in range(8):
            P.op("pe", lambda e, c=c, hs=hs, ntok=ntok: e.matmul(bank(3)[0:64, :ntok], Wkr[:, c, :],
                                                                hT[hs][:, c, :ntok], start=(c == 0), stop=(c == 7)),
                 reads=["Wkr", ("hT", hs)], writes=[pb(3)])
        for t in range(min(2, ntile)):
            nt = min(128, ntok - t * 128)
            fb = t % 2
            lb = tb * 4 + t
            for c in range(8):
                P.op("pe", lambda e, c=c, hs=hs, t=t, nt=nt, fb=fb: e.matmul(
                    bank(fb)[:nt, :], hT[hs][:, c, t * 128:t * 128 + nt], Win[:, c, 416:928],
                    start=(c == 0), stop=(c == 7)),
                     reads=["Win", ("hT", hs)], writes=[pb(fb)])
            P.op("dve", lambda e, nt=nt, fb=fb, lb=lb: e.tensor_copy(out=f_sb[:nt, lb, :], in_=bank(fb)[:nt, :]),
                 reads=[pb(fb)], writes=[("f", lb)])
            if 32 <= lb < 64:
                b = lb - 32
                fs_ = b % 2
                P.op("dve", lambda e, b=b, fs_=fs_: e.tensor_tensor(out=ftmp[fs_], in0=f_sb[:, b, :],
                                                                  in1=f_sb[:, b + 32, :], op=ALU.subtract),
                     reads=[("f", b), ("f", b + 32)], writes=[("ftmp", fs_)])
                P.op("dve", lambda e, b=b: e.tensor_tensor(out=f_sb[:, b, :], in0=f_sb[:, b, :],
                                                           in1=f_sb[:, b + 32, :], op=ALU.add),
                     reads=[("f", b), ("f", b + 32)], writes=[("f", b)])
                P.op("pool", lambda e, b=b, fs_=fs_: e.tensor_copy(out=f_sb[:, b + 32, :], in_=ftmp[fs_]),
                     reads=[("ftmp", fs_)], writes=[("f", b + 32)])

        P.op("pe", lambda e, ntok=ntok: e.matmul(bank(4)[:, :ntok], ones, sq[:, 0, :ntok], start=True, stop=True),
             reads=["ones", ("sq", 0)], writes=[pb(4)])
        if own:
            for m in range(2):
                P.op("pe", lambda e, m=m: e.matmul(bank(5), ones, sq[:, 1 + m, :], start=(m == 0), stop=(m == 1)),
                     reads=["ones", ("sq", 1 + m)], writes=[pb(5)])
        P.op("act", lambda e, ntok=ntok: e.activation(out=lnb[:, 0, :ntok], in_=bank(4)[:, :ntok], func=AF.Ln,
                                                      scale=1.0 / 128, bias=epst),
             reads=[pb(4), "eps"], writes=[("lnb", 0)])
        P.op("act", lambda e, ntok=ntok: e.activation(out=rbc[:, 0, :ntok], in_=lnb[:, 0, :ntok], func=AF.Exp,
                                                      scale=-0.5),
             reads=[("lnb", 0)], writes=[("rbc", 0)])
        if own:
            P.op("act", lambda e: e.activation(out=lnb[:, 1, :], in_=bank(5), func=AF.Ln, scale=1.0 / 256, bias=epst),
                 reads=[pb(5), "eps"], writes=[("lnb", 1)])
            P.op("act", lambda e: e.activation(out=rbc[:, 1, :], in_=lnb[:, 1, :], func=AF.Exp, scale=-0.5),
                 reads=[("lnb", 1)], writes=[("rbc", 1)])
        for t in range(2, ntile):
            nt = min(128, ntok - t * 128)
            fb = t % 2
            lb = tb * 4 + t
            for c in range(8):
                P.op("pe", lambda e, c=c, hs=hs, t=t, nt=nt, fb=fb: e.matmul(
                    bank(fb)[:nt, :], hT[hs][:, c, t * 128:t * 128 + nt], Win[:, c, 416:928],
                    start=(c == 0), stop=(c == 7)),
                     reads=["Win", ("hT", hs)], writes=[pb(fb)])
            P.op("dve", lambda e, nt=nt, fb=fb, lb=lb: e.tensor_copy(out=f_sb[:nt, lb, :], in_=bank(fb)[:nt, :]),
                 reads=[pb(fb)], writes=[("f", lb)])
            if 32 <= lb < 64:
                b = lb - 32
                fs_ = b % 2
                P.op("dve", lambda e, b=b, fs_=fs_: e.tensor_tensor(out=ftmp[fs_], in0=f_sb[:, b, :],
                                                                  in1=f_sb[:, b + 32, :], op=ALU.subtract),
                     reads=[("f", b), ("f", b + 32)], writes=[("ftmp", fs_)])
                P.op("dve", lambda e, b=b: e.tensor_tensor(out=f_sb[:, b, :], in0=f_sb[:, b, :],
                                                           in1=f_sb[:, b + 32, :], op=ALU.add),
                     reads=[("f", b), ("f", b + 32)], writes=[("f", b)])
                P.op("pool", lambda e, b=b, fs_=fs_: e.tensor_copy(out=f_sb[:, b + 32, :], in_=ftmp[fs_]),
                     reads=[("ftmp", fs_)], writes=[("f", b + 32)])

        P.op("dve", lambda e, t0=t0, ntok=ntok: e.scalar_tensor_tensor(
            out=ckvT[:, t0:t0 + ntok], in0=bank(2)[:, :ntok], scalar=gkv[:, 0:1], in1=rbc[:, 0, :ntok],
            op0=ALU.mult, op1=ALU.mult),
             reads=[pb(2), "gkv", ("rbc", 0)], writes=[("ckvT", tb)])
        if own:
            for m in range(2):
                P.op("dve", lambda e, m=m, t0=t0: e.scalar_tensor_tensor(
                    out=cqT[:, m, t0:t0 + 512], in0=bank(6 + m), scalar=gq[:, m:m + 1], in1=rbc[:, 1, :],
                    op0=ALU.mult, op1=ALU.mult),
                     reads=[pb(6 + m), "gq", ("rbc", 1)], writes=[("cqT", tb)])
        P.op("dve", lambda e, ks=ks, ntok=ntok: e.tensor_tensor(out=krp[0:32, :ntok], in0=bank(3)[0:32, :ntok],
                                                               in1=ktb[ks][0:32, :ntok], op=ALU.mult),
             reads=[pb(3), ("ktb", ks)], writes=["krp"])
        P.op("dve", lambda e, ks=ks, ntok=ntok: e.tensor_tensor(out=krp2[0:32, :ntok], in0=bank(3)[32:64, :ntok],
                                                               in1=ktb[ks][32:64, :ntok], op=ALU.mult),
             reads=[pb(3), ("ktb", ks)], writes=["krp2"])
        P.op("dve", lambda e, t0=t0, ntok=ntok: e.tensor_tensor(out=kropeT[:, t0:t0 + ntok], in0=krp[0:32, :ntok],
                                                               in1=krp2[0:32, :ntok], op=ALU.add),
             reads=["krp", "krp2"], writes=[("kropeT", tb)])
    dma("sp", ktb[0], d_ktab[:, 0:512], ("ktb", 0), writes=[("ktb", 0)])
    p_t1(0)
    p_t1(1)
    for i in range(len(ptiles)):
        if i + 2 < len(ptiles):
            p_t1(i + 2)
        p_t2(i)
        if i + 1 == len(ptiles) or ptiles[i + 1][0] != ptiles[i][0]:
            p_proj(ptiles[i][0])
    A.release("Win", "Wkr", "xt0", "xt1", "xt2", "hb0", "hb1", "hb2", "hb3", "hT0", "hT1", "junk", "sq", "lnb", "rbc",
              "ktb0", "ktb1", "krp", "krp2", "ftmp0", "ftmp1", "ccs", "wfs", "wst0", "wst1")
    P.barrier()

    if stop_after == 'P':
        return _finish(nc, P)
    fNb = [A.tile("fNb%d" % i, [128, 4, 512], BF16) for i in range(2)]
    tabs = [A.tile("tab%d" % i, [128, 8, 512], BF16) for i in range(3)]
    ABs = A.tile("ABs", [128, 8, 512], BF16)
    ysb = A.tile("ysb", [128, 4, 512], F32)
    ysq = A.tile("ysq", [128, 4, 512], BF16)
    lnf = A.tile("lnf", [128, 512], F32)
    rbf = A.tile("rbf", [128, 512], F32)
    chunk_state = [0]

    def f_load(kb, ch):
        ts_ = chunk_state[0] % 3
        chunk_state[0] += 1
        dma("sp", tabs[ts_], d_dft[kb, :, ch, :, :], ("tab", ts_), writes=[("tab", ts_)])
        return ts_

    def f_mm_a(ch, ts_):
        for j in range(4):
            b = ch * 4 + j
            for g in range(4):
                P.op("pe", lambda e, j=j, b=b, g=g: e.matmul(
                    bank(g), f_sb[:, b, g * 128:(g + 1) * 128], tabs[ts_][:, 2 * j, :],
                    start=(b == 0), stop=False),
                     reads=[("f", b), ("tab", ts_)], writes=[pb(g)])

    def f_mm_b(ch, ts_, groups):
        for j in range(4):
            b = ch * 4 + j
            for g in groups:
                P.op("pe", lambda e, j=j, b=b, g=g: e.matmul(
                    bank(4 + g), f_sb[:, 32 + b, g * 128:(g + 1) * 128], tabs[ts_][:, 2 * j + 1, :],
                    start=(b == 0), stop=False),
                     reads=[("f", 32 + b), ("tab", ts_)], writes=[pb(4 + g)])

    def f_mm_edge(ts_):
        for g in range(4):
            P.op("pe", lambda e, g=g: e.matmul(
                bank(g), f_sb[:, 63, g * 128:(g + 1) * 128], tabs[ts_][:, 0, :], start=False, stop=False),
                 reads=[("f", 63), ("tab", ts_)], writes=[pb(g)])
            P.op("pe", lambda e, g=g: e.matmul(
                bank(4 + g), f_sb[:, 31, g * 128:(g + 1) * 128], tabs[ts_][:, 1, :], start=False, stop=False),
                 reads=[("f", 31), ("tab", ts_)], writes=[pb(4 + g)])
            P.op("pe", lambda e, g=g: e.matmul(
                bank(g), f_sb[:16, 64, g * 128:(g + 1) * 128], tabs[ts_][:16, 2, :], start=False, stop=True),
                 reads=[("f", 64), ("tab", ts_)], writes=[pb(g)])
            P.op("pe", lambda e, g=g: e.matmul(
                bank(4 + g), f_sb[:16, 64, g * 128:(g + 1) * 128], tabs[ts_][:16, 3, :], start=False, stop=True),
                 reads=[("f", 64), ("tab", ts_)], writes=[pb(4 + g)])

    def f_tail_evac():
        for g in range(4):
            P.op("act", lambda e, g=g: e.copy(out=ABs[:, g, :], in_=bank(g)), reads=[pb(g)], writes=[("ABs", g)])
            P.op("dve", lambda e, g=g: e.tensor_copy(out=ABs[:, 4 + g, :], in_=bank(4 + g)), reads=[pb(4 + g)],
                 writes=[("ABs", 4 + g)])

    def f_tail_y():
        for g in range(4):
            P.op("pe", lambda e, g=g: e.matmul(bank(4 + g), M12[:, g, :], ABs[:, g, :], start=True, stop=False),
                 reads=["M12", ("ABs", g)], writes=[pb(4 + g)])
            P.op("pe", lambda e, g=g: e.matmul(bank(4 + g), M12[:, 4 + g, :], ABs[:, 4 + g, :], start=False,
                                               stop=True),
                 reads=["M12", ("ABs", 4 + g)], writes=[pb(4 + g)])
            P.op("act", lambda e, g=g: e.activation(out=ysb[:, g, :], in_=bank(4 + g), func=AF.Identity,
                                                    bias=bfb[:, g:g + 1]),
                 reads=[pb(4 + g), "bfb"], writes=[("ysb", g)])
            P.op("dve", lambda e, g=g: e.tensor_tensor(out=ysq[:, g, :], in0=ysb[:, g, :], in1=ysb[:, g, :],
                                                       op=ALU.mult),
                 reads=[("ysb", g)], writes=[("ysq", g)])

    def f_tail_stats(kb):
        for g in range(4):
            P.op("pe", lambda e, g=g: e.matmul(bank(7), ones, ysq[:, g, :], start=(g == 0), stop=(g == 3)),
                 reads=["ones", ("ysq", g)], writes=[pb(7)])
        P.op("act", lambda e: e.activation(out=lnf, in_=bank(7), func=AF.Ln, scale=1.0 / 512, bias=epst),
             reads=[pb(7), "eps"], writes=["lnf"])
        P.op("act", lambda e: e.activation(out=rbf, in_=lnf, func=AF.Exp, scale=-0.5), reads=["lnf"], writes=["rbf"])
        fs = kb % 2
        for g in range(4):
            P.op("dve", lambda e, g=g: e.scalar_tensor_tensor(
                out=fNb[fs][:, g, :], in0=ysb[:, g, :], scalar=gf[:, g:g + 1], in1=rbf,
                op0=ALU.mult, op1=ALU.mult),
                 reads=[("ysb", g), "gf", "rbf"], writes=[("fNb", fs)])
        dma("pool", scr_f[kb], fNb[fs], ("fNout", fs), reads=[("fNb", fs)], writes=["scr_f"])

    for kb in range(NQB):
        ts0 = f_load(kb, 0)
        if kb == 0:
            f_mm_a(0, ts0)
            f_mm_b(0, ts0, [0, 1, 2, 3])
        else:
            f_mm_a(0, ts0)
            f_tail_y()
            f_mm_b(0, ts0, [0, 1, 2])
            f_tail_stats(kb - 1)
            f_mm_b(0, ts0, [3])
        for ch in range(1, 8):
            ts_ = f_load(kb, ch)
            f_mm_a(ch, ts_)
            f_mm_b(ch, ts_, [0, 1, 2, 3])
        ts_ = f_load(kb, 8)
        f_mm_edge(ts_)
        f_tail_evac()
    f_tail_y()
    f_tail_stats(NQB - 1)
    A.release("f_sb", "tab0", "tab1", "tab2", "ABs", "ysb", "ysq", "lnf", "rbf", "fNb0", "fNb1")
    P.barrier()

    if stop_after == 'F':
        return _finish(nc, P)
    attnT = A.tile("attnT", [128, 4, NOWN], BF16)
    KT = [A.tile("KT%d" % i, [96, L], BF16) for i in range(2)]
    VA = [A.tile("VA%d" % i, [128, NLB, 128], BF16) for i in range(2)]
    QT = [A.tile("QT%d" % i, [96, 512], BF16) for i in range(2)]
    PT = [A.tile("PT%d" % i, [128, 1024], BF16) for i in range(3)]
    qtab = A.tile("qtab", [128, NOWN], F32)
    tq = A.tile("tq", [128, 2, 512], F32)
    rcp = A.tile("rcp", [64, 512], F32)
    stg = [A.tile("stg%d" % i, [128, 1024], BF16) for i in range(4)]
    dma("sp", qtab, d_qtab, "qtab", writes=["qtab"])
    for i in range(2):
        P.op("dve", lambda e, i=i: e.memset(VA[i][:, :, 64:128], 1.0), writes=[("VAones", i)])

    wg_r = w_gate.rearrange("(c p) (j n) -> p j c n", p=128, n=128)
    wu_r = w_up.rearrange("(c p) (j n) -> p j c n", p=128, n=128)
    wd_r = w_down.rearrange("(j p) n -> p j n", p=128)
    sctr = 0
    for j in range(NJ):
        for src, dst, is3 in [(wg_r, scr_g, True), (wu_r, scr_u, True), (wd_r, scr_d, False)]:
            s = sctr % 4
            sctr += 1
            if is3:
                dma("pool", stg[s].rearrange("p (c n) -> p c n", n=128), src[:, j, :, :], ("stgin", s),
                    writes=[("stg", s)])
            else:
                dma("pool", stg[s], src[:, j, :], ("stgin", s), writes=[("stg", s)])
            dma("sp", dst[j], stg[s], ("stgout", s), reads=[("stg", s)], writes=["scr"])

    osb = A.tile("osb", [128, 512], F32)
    JB = 7
    OB = 6

    def kv_steps(h, banks=(7,)):
        kv = h % 2
        steps = []
        bctr = [0]
        for tb in range(NTB):
            def st(tb=tb):
                t0 = tb * 512
                ntok = 512 if tb < 16 else 16
                JB = banks[bctr[0] % len(banks)]
                bctr[0] += 1
                P.op("pe", lambda e: e.matmul(bank(JB)[0:64, :ntok], Wkvb[:, h * 128:h * 128 + 64],
                                              ckvT[:, t0:t0 + ntok], start=True, stop=True),
                     reads=["Wkvb", ("ckvT", tb)], writes=[pb(JB)])
                P.op("dve", lambda e: e.tensor_copy(out=KT[kv][0:64, t0:t0 + ntok], in_=bank(JB)[0:64, :ntok]),
                     reads=[pb(JB)], writes=[("KT", kv)])
                if h < 2:
                    P.op("dve", lambda e: e.tensor_copy(out=KT[kv][64:96, t0:t0 + ntok],
                                                        in_=kropeT[:, t0:t0 + ntok]),
                         reads=[("kropeT", tb)], writes=[("KT", kv)])
            steps.append(st)
        for l0 in range(0, NLB, 8):
            def st(l0=l0):
                nl = min(8, NLB - l0)
                JB = banks[bctr[0] % len(banks)]
                bctr[0] += 1
                for li in range(nl):
                    lb = l0 + li
                    kp = 128 if lb < 64 else 16
                    P.op("pe", lambda e, lb=lb, li=li, kp=kp: e.matmul(
                        bank(JB)[:kp, li * 64:(li + 1) * 64], ckvT[:, lb * 128:lb * 128 + kp],
                        Wkvb[:, h * 128 + 64:h * 128 + 128], start=True, stop=True),
                         reads=["Wkvb"] + [("ckvT", lb // 4)], writes=[pb(JB)])
                if l0 + nl <= 64:
                    P.op("dve", lambda e: e.tensor_copy(
                        out=VA[kv][:, l0:l0 + nl, 0:64],
                        in_=bank(JB).rearrange("p (l d) -> p l d", d=64)[:, 0:nl, :]),
                         reads=[pb(JB)], writes=[("VA", kv)])
                else:
                    P.op("dve", lambda e: e.tensor_copy(out=VA[kv][0:16, l0, 0:64], in_=bank(JB)[0:16, 0:64]),
                         reads=[pb(JB)], writes=[("VA", kv)])
            steps.append(st)
        return steps

    def q_build(h, qb, qs):
        q0 = qb * 512
        for c in range(2):
            P.op("pe", lambda e, c=c: e.matmul(bank(JB), Wq[:, c, h, :], cqT[:, c, q0:q0 + 512],
                                               start=(c == 0), stop=(c == 1)),
                 reads=[("Wq", c, 0), ("Wq", c, 1), ("Wq", c, 2), ("cqT", qb)], writes=[pb(JB)])
        P.op("dve", lambda e: e.tensor_copy(out=QT[qs][0:64, :], in_=bank(JB)[0:64, :]),
             reads=[pb(JB)], writes=[("QT", qs)])
        P.op("dve", lambda e: e.tensor_tensor(out=tq[64:96, 0, :], in0=bank(JB)[64:96, :],
                                              in1=qtab[64:96, q0:q0 + 512], op=ALU.mult),
             reads=[pb(JB), "qtab"], writes=["tq0"])
        P.op("dve", lambda e: e.tensor_tensor(out=tq[64:96, 1, :], in0=bank(JB)[96:128, :],
                                              in1=qtab[96:128, q0:q0 + 512], op=ALU.mult),
             reads=[pb(JB), "qtab"], writes=["tq1"])
        P.op("dve", lambda e: e.tensor_tensor(out=QT[qs][64:96, :], in0=tq[64:96, 0, :], in1=tq[64:96, 1, :],
                                              op=ALU.add),
             reads=["tq0", "tq1"], writes=[("QT", qs)])

    items = [(h, qb) for h in range(H) for qb in range(NQB)]
    for st in kv_steps(0, banks=(7, 6, 5, 4)):
        st()
    q_build(0, 0, 0)
    pending = []
    def attn_item(it, h, qb, pending):
        kv = h % 2
        q0 = qb * 512
        qs = it % 2

        def mm_s(kg):
            g = kg % 3
            for i in range(2):
                lb = kg * 2 + i
                P.op("pe", lambda e, lb=lb, i=i, g=g: e.matmul(
                    bank(g * 2 + i), KT[kv][:, lb * 128:(lb + 1) * 128], QT[qs], start=True, stop=True),
                     reads=[("KT", kv), ("QT", qs)], writes=[pb(g * 2 + i)])

        mm_s(0)
        mm_s(1)
        for kg in range(32):
            g = kg % 3
            ps_ = kg % 3
            if kg + 2 < 32:
                mm_s(kg + 2)
            elif kg + 2 == 32:
                gm = 32 % 3
                P.op("pe", lambda e, gm=gm: e.matmul(bank(gm * 2)[0:16, :], KT[kv][:, 8192:8208], QT[qs],
                                                     start=True, stop=True),
                     reads=[("KT", kv), ("QT", qs)], writes=[pb(gm * 2)])
            P.op("act", lambda e, g=g, ps_=ps_: e.activation(out=PT[ps_], in_=bank(g * 2, 2), func=AF.Exp,
                                                            scale=SCALE),
                 reads=[pb(g * 2), pb(g * 2 + 1)], writes=[("PT", ps_)])
            for i in range(2):
                lb = kg * 2 + i
                P.op("pe", lambda e, lb=lb, i=i, ps_=ps_: e.matmul(
                    bank(OB), VA[kv][:, lb, :], PT[ps_][:, i * 512:(i + 1) * 512], start=(lb == 0), stop=False),
                     reads=[("VA", kv), ("VAones", kv), ("PT", ps_)], writes=[pb(OB)])
            if kg == 6 and it + 1 < len(items):
                q_build(items[it + 1][0], items[it + 1][1], (it + 1) % 2)
            if kg >= 8 and kg % 4 == 0 and pending and qb >= 1:
                pending.pop(0)()
        gm = 32 % 3
        pm = 32 % 3
        P.op("act", lambda e, gm=gm, pm=pm: e.activation(out=PT[pm][0:16, 0:512], in_=bank(gm * 2)[0:16, :],
                                                        func=AF.Exp, scale=SCALE),
             reads=[pb(gm * 2)], writes=[("PT", pm)])
        P.op("pe", lambda e, pm=pm: e.matmul(bank(OB), VA[kv][0:16, 64, :], PT[pm][0:16, 0:512],
                                             start=False, stop=True),
             reads=[("VA", kv), ("VAones", kv), ("PT", pm)], writes=[pb(OB)])
        P.op("dve", lambda e: e.tensor_copy(out=osb, in_=bank(OB)), reads=[pb(OB)], writes=["osb"])
        P.op("dve", lambda e: e.reciprocal(out=rcp, in_=osb[64:128, :]), reads=["osb"], writes=["rcp"])
        po = (h % 2) * 64
        P.op("dve", lambda e, h=h, po=po, q0=q0: e.tensor_tensor(
            out=attnT[po:po + 64, h // 2, q0:q0 + 512], in0=osb[0:64, :], in1=rcp, op=ALU.mult),
             reads=["osb", "rcp"], writes=["attnT"])
        if qb == NQB - 1:
            while pending:
                pending.pop(0)()
    for it, (h, qb) in enumerate(items):
        if qb == 0 and h + 1 < H:
            pending = kv_steps(h + 1)
        attn_item(it, h, qb, pending)
    A.release("KT0", "KT1", "VA0", "VA1", "QT0", "QT1", "PT0", "PT1", "PT2", "qtab", "tq", "rcp", "osb",
              "stg0", "stg1", "stg2", "stg3", "ckvT", "kropeT", "cqT")
    P.barrier()

    if stop_after == 'A':
        return _finish(nc, P)
    hfT = A.tile("hfT", [128, 8, NOWN], BF16)
    Wo = A.tile("Wo", [128, 8, D], BF16)
    gpost = A.tile("gpost", [128, D], F32)
    gpreffn = A.tile("gpreffn", [128, D], F32)
    mT = [A.tile("mT%d" % i, [128, 4, 512], BF16) for i in range(2)]
    fN = [A.tile("fN%d" % i, [128, 4, 512], BF16) for i in range(2)]
    asq = A.tile("asq", [128, 4, 512], BF16)
    lna = A.tile("lna", [128, 512], F32)
    rba = A.tile("rba", [128, 512], F32)
    NOB = 4
    xo = [A.tile("xo%d" % i, [128, D], F32) for i in range(NOB)]
    r1 = [A.tile("r1%d" % i, [128, D], F32) for i in range(NOB)]
    hfb = [A.tile("hfb%d" % i, [128, D], BF16) for i in range(NOB)]
    junk2 = A.tile("junk2", [128, D], BF16)
    junk2b = A.tile("junk2b", [128, D], BF16)
    ssq2 = A.tile("ssq2", [128, 8], F32)
    dma("pool", Wo, w_o.rearrange("(c p) n -> p c n", p=128), "wo", writes=["Wo"])
    dma("sp", gpost, d_gpost.partition_broadcast(128), "gp_a", writes=["gpost"])
    dma("sp", gpreffn, d_gpreffn.partition_broadcast(128), "gp_b", writes=["gpreffn"])

    def rstd2(col, n):
        P.op("act", lambda e: e.activation(out=ssq2[:, col:col + 1], in_=ssq2[:, col:col + 1], func=AF.Ln,
                                           scale=1.0 / n, bias=epst),
             reads=[("ssq2", col), "eps"], writes=[("ssq2", col)])
        P.op("act", lambda e: e.activation(out=ssq2[:, col:col + 1], in_=ssq2[:, col:col + 1], func=AF.Exp,
                                           scale=-0.5),
             reads=[("ssq2", col)], writes=[("ssq2", col)])

    def o_prefix(qb):
        q0 = qb * 512
        fs = qb % 2
        dma("sp", fN[fs], scr_f[qb], ("fNin", fs), reads=["scr_f"], writes=[("fN", fs)])
        for pr in range(4):
            P.op("act", lambda e, pr=pr: e.activation(out=asq[:, pr, :], in_=attnT[:, pr, q0:q0 + 512],
                                                      func=AF.Square),
                 reads=["attnT"], writes=[("asq", pr)])
            P.op("pe", lambda e, pr=pr: e.matmul(bank(6), ones, asq[:, pr, :], start=(pr == 0), stop=(pr == 3)),
                 reads=["ones", ("asq", pr)], writes=[pb(6)])
        P.op("act", lambda e: e.activation(out=lna, in_=bank(6), func=AF.Ln, scale=1.0 / 512, bias=epst),
             reads=[pb(6), "eps"], writes=["lna"])
        P.op("act", lambda e: e.activation(out=rba, in_=lna, func=AF.Exp, scale=-0.5), reads=["lna"], writes=["rba"])
        for pr in range(4):
            P.op("dve", lambda e, pr=pr: e.scalar_tensor_tensor(
                out=mT[fs][:, pr, :], in0=attnT[:, pr, q0:q0 + 512], scalar=ga[:, pr:pr + 1], in1=rba,
                op0=ALU.mult, op1=ALU.mult),
                 reads=["attnT", "ga", "rba"], writes=[("mT", fs, pr)])

    def o_tile_a(qb, t, tctr):
        q0 = qb * 512
        fs = qb % 2
        s = tctr % NOB
        pbk = (tctr % 2) * 2
        c0 = (tctr % NOB) * 2
        r0 = q0 + t * 128
        if tctr + 3 < len(otiles):
            qn, tn = otiles[tctr + 3]
            rn = qn * 512 + tn * 128
            sn = (tctr + 3) % NOB
            dma("sp", xo[sn], xperm[rn:rn + 128, :], ("xo", sn), writes=[("xo", sn)])
        for hf in range(2):
            bk = pbk + hf
            for c in range(8):
                lhs = (lambda c=c: mT[fs][:, c, t * 128:(t + 1) * 128]) if c < 4 else \
                    (lambda c=c: fN[fs][:, c - 4, t * 128:(t + 1) * 128])
                P.op("pe", lambda e, lhs=lhs, c=c, hf=hf, bk=bk: e.matmul(
                    bank(bk), lhs(), Wo[:, c, hf * 512:(hf + 1) * 512], start=(c == 0), stop=(c == 7)),
                     reads=["Wo", ("fN", fs)] + [("mT", fs, c) for c in range(4)], writes=[pb(bk)])
        P.op("act", lambda e: e.activation(out=junk2, in_=bank(pbk, 2), func=AF.Square,
                                           accum_out=ssq2[:, c0:c0 + 1]),
             reads=[pb(pbk), pb(pbk + 1)], writes=["junk2", ("ssq2", c0)])
        rstd2(c0, D)
        P.op("dve", lambda e: e.scalar_tensor_tensor(
            out=r1[s], in0=bank(pbk, 2), scalar=ssq2[:, c0:c0 + 1], in1=gpost, op0=ALU.mult, op1=ALU.mult),
             reads=[pb(pbk), pb(pbk + 1), ("ssq2", c0), "gpost"], writes=[("r1", s)])
        P.op("pool", lambda e: e.tensor_tensor(out=r1[s], in0=r1[s], in1=xo[s], op=ALU.add),
             reads=[("r1", s), ("xo", s)], writes=[("r1", s)])
        dma("pool", y[r0:r0 + 128, :], r1[s], ("h1out", s), reads=[("r1", s)], writes=[("yrow", r0)])

    def o_tile_b(qb, t, tctr):
        q0 = qb * 512
        s = tctr % NOB
        c0 = (tctr % NOB) * 2
        r0 = q0 + t * 128
        P.op("act", lambda e: e.activation(out=junk2b, in_=r1[s], func=AF.Square,
                                           accum_out=ssq2[:, c0 + 1:c0 + 2]),
             reads=[("r1", s)], writes=["junk2b", ("ssq2", c0 + 1)])
        rstd2(c0 + 1, D)
        P.op("dve", lambda e: e.scalar_tensor_tensor(
            out=hfb[s], in0=r1[s], scalar=ssq2[:, c0 + 1:c0 + 2], in1=gpreffn, op0=ALU.mult, op1=ALU.mult),
             reads=[("r1", s), ("ssq2", c0 + 1), "gpreffn"], writes=[("hfb", s)])

    def o_tile_b2(qb, t, tctr):
        q0 = qb * 512
        s = tctr % NOB
        r0 = q0 + t * 128
        tbk = 4 + (tctr % 2)
        for c in range(8):
            P.op("pe", lambda e, c=c: e.transpose(bank_bf(tbk)[:, c * 128:(c + 1) * 128],
                                                  hfb[s][:, c * 128:(c + 1) * 128], ident),
                 reads=[("hfb", s), "ident"], writes=[pb(tbk)])
        P.op("dve", lambda e: e.tensor_copy(
            out=hfT[:, :, r0:r0 + 128], in_=bank_bf(tbk).rearrange("p (c n) -> p c n", n=128)),
             reads=[pb(tbk)], writes=[("hfT", qb)])

    otiles = [(qb, t) for qb in range(NQB) for t in range(4)]
    for i0 in range(3):
        dma("sp", xo[i0], xperm[i0 * 128:(i0 + 1) * 128, :], ("xo", i0), writes=[("xo", i0)])
    o_prefix(0)
    o_tile_a(otiles[0][0], otiles[0][1], 0)
    for i in range(len(otiles) + 1):
        if i < len(otiles):
            qb, t = otiles[i]
            if t == 0 and qb + 1 < NQB:
                o_prefix(qb + 1)
        if i + 1 < len(otiles):
            o_tile_a(otiles[i + 1][0], otiles[i + 1][1], i + 1)
        if i < len(otiles):
            o_tile_b(otiles[i][0], otiles[i][1], i)
        if i >= 1:
            o_tile_b2(otiles[i - 1][0], otiles[i - 1][1], i - 1)
    A.release("Wo", "gpost", "gpreffn", "mT0", "mT1", "asq", "lna", "rba", "xo0", "xo1", "xo2", "xo3", "r10", "r11",
              "r12", "r13", "hfb0", "hfb1", "hfb2", "hfb3",
              "junk2", "junk2b", "attnT", "fN0", "fN1")
    P.barrier()

    if stop_after == 'O':
        return _finish(nc, P)
    Wd = A.tile("Wd", [128, NJ, D], BF16)
    act_sb = A.tile("act_sb", [128, NJ, 512], BF16)
    wgu = [A.tile("wgu%d" % i, [128, 2, 8, 128], BF16) for i in range(3)]
    sg = [A.tile("sg%d" % i, [128, 512], F32) for i in range(2)]
    gpostffn = A.tile("gpostffn", [128, D], F32)
    h1 = [A.tile("h1%d" % i, [128, D], F32) for i in range(2)]
    fo = [A.tile("fo%d" % i, [128, D], F32) for i in range(2)]
    junk3 = A.tile("junk3", [128, D], BF16)
    dma("sp", gpostffn, d_gpostffn.partition_broadcast(128), "gp2", writes=["gpostffn"])
    wctr = 0
    tctr = 0
    for qb in range(NQB):
        q0 = qb * 512
        for j in range(NJ):
            ws = wctr % 3
            gb = (wctr % 2) * 2
            ss = wctr % 2
            wctr += 1
            dma("sp", wgu[ws][:, 0, :, :].rearrange("p c n -> p (c n)"), scr_g[j], ("wgu", ws), reads=["scr"],
                writes=[("wgu", ws)])
            dma("sp", wgu[ws][:, 1, :, :].rearrange("p c n -> p (c n)"), scr_u[j], ("wgu", ws), reads=["scr"],
                writes=[("wgu", ws)])
            if qb == 0:
                dma("sp", Wd[:, j, :], scr_d[j], ("wd", j), reads=["scr"], writes=[("Wd", j)])
            for gu in range(2):
                for c in range(8):
                    P.op("pe", lambda e, gu=gu, c=c, ws=ws, gb=gb, q0=q0: e.matmul(
                        bank(gb + gu), wgu[ws][:, gu, c, :], hfT[:, c, q0:q0 + 512], start=(c == 0), stop=(c == 7)),
                         reads=[("wgu", ws), ("hfT", qb)], writes=[pb(gb + gu)])
            P.op("act", lambda e, gb=gb, ss=ss: e.activation(out=sg[ss], in_=bank(gb), func=AF.Silu),
                 reads=[pb(gb)], writes=[("sg", ss)])
            P.op("dve", lambda e, gb=gb, ss=ss, j=j: e.tensor_tensor(out=act_sb[:, j, :], in0=sg[ss],
                                                                    in1=bank(gb + 1), op=ALU.mult),
                 reads=[("sg", ss), pb(gb + 1)], writes=[("act", j)])
        for t in range(4):
            s = tctr % 2
            tctr += 1
            r0 = q0 + t * 128
            dma("sp", h1[s], y[r0:r0 + 128, :], ("h1in", s), reads=[("yrow", r0)], writes=[("h1", s)])
            for hf in range(2):
                bk = 4 + 2 * s + hf
                for j in range(NJ):
                    P.op("pe", lambda e, j=j, t=t, hf=hf, bk=bk: e.matmul(
                        bank(bk), act_sb[:, j, t * 128:(t + 1) * 128], Wd[:, j, hf * 512:(hf + 1) * 512],
                        start=(j == 0), stop=(j == NJ - 1)),
                         reads=[("Wd", j), ("act", j)], writes=[pb(bk)])
            bk0 = 4 + 2 * s
            col = 4 + s
            P.op("act", lambda e, bk0=bk0, col=col: e.activation(out=junk3, in_=bank(bk0, 2), func=AF.Square,
                                                                accum_out=ssq2[:, col:col + 1]),
                 reads=[pb(bk0), pb(bk0 + 1)], writes=["junk3", ("ssq2", col)])
            rstd2(col, D)
            P.op("dve", lambda e, s=s, bk0=bk0, col=col: e.scalar_tensor_tensor(
                out=fo[s], in0=bank(bk0, 2), scalar=ssq2[:, col:col + 1], in1=gpostffn, op0=ALU.mult, op1=ALU.mult),
                 reads=[pb(bk0), pb(bk0 + 1), ("ssq2", col), "gpostffn"], writes=[("fo", s)])
            P.op("dve", lambda e, s=s: e.tensor_tensor(out=fo[s], in0=fo[s], in1=h1[s], op=ALU.add),
                 reads=[("fo", s), ("h1", s)], writes=[("fo", s)])
            dma("pool", y[r0:r0 + 128, :], fo[s], ("yout", s), reads=[("fo", s), ("h1", s)], writes=[("yrow", r0)])
    return _finish(nc, P)


def _finish(nc, P):
    P.barrier()
    for e in ENGS:
        P.op(e, lambda eng: eng.nop(), reads=(), writes=())
    with nc.Block() as block:
        P.emit(nc, {"pe": block.tensor, "act": block.scalar, "dve": block.vector, "pool": block.gpsimd,
                    "sp": block.sync})
    return nc


_CACHE = {}


def _host_tables():
    if "tabs" in _CACHE:
        return _CACHE["tabs"]
    out = {}
    inv_freq = (10000.0 ** (-np.arange(0, 32, 2, dtype=np.float32) / 32)).astype(np.float32)
    for hf in range(2):
        own = (NMETA + hf * NOWN + np.arange(NOWN)).astype(np.int64)
        oth_set = set((NMETA + (1 - hf) * NOWN + np.arange(NOWN)).tolist())
        npair = NOWN - 15
        paired = (L - own[:npair])
        assert all(int(p) in oth_set for p in paired)
        left = np.array(sorted(oth_set - set(paired.tolist())), np.int64)
        other = np.concatenate([paired, left])
        assert other.shape[0] == NOWN
        pos = np.concatenate([own, other, np.arange(NMETA)]).astype(np.int64)
        ang = pos.astype(np.float32)[:, None] * inv_freq[None, :]
        c = np.cos(ang).astype(np.float32).T
        s = np.sin(ang).astype(np.float32).T
        ktab = np.concatenate([c, c, -s, s], axis=0).astype(np.float32)
        qtab = np.zeros((128, NOWN), np.float32)
        qtab[64:96] = np.concatenate([c, c], axis=0)[:, :NOWN]
        qtab[96:128] = np.concatenate([-s, s], axis=0)[:, :NOWN]

        def cs(rows):
            th = ((rows[:, None] * own[None, :]) % L).astype(np.float64) * (2.0 * np.pi / L)
            return np.cos(th), np.sin(th)

        c_own, s_own = cs(own)
        c_oth, s_oth = cs(other)
        Ce = (0.5 * (c_own + c_oth)).reshape(32, 128, NQB, 512)
        So = (0.5 * (s_own - s_oth)).reshape(32, 128, NQB, 512)
        Co = (0.5 * (c_own - c_oth)).reshape(32, 128, NQB, 512)[31]
        Se = (0.5 * (s_own + s_oth)).reshape(32, 128, NQB, 512)[31]
        del c_own, s_own, c_oth, s_oth
        cm, sm = cs(np.arange(NMETA, dtype=np.int64))
        tab = np.zeros((NQB, 128, NCHK, 8, 512), ml_dtypes.bfloat16)
        for ch in range(8):
            for j in range(4):
                b = ch * 4 + j
                tab[:, :, ch, 2 * j, :] = Ce[b].transpose(1, 0, 2).astype(ml_dtypes.bfloat16)
                tab[:, :, ch, 2 * j + 1, :] = So[b].transpose(1, 0, 2).astype(ml_dtypes.bfloat16)
        tab[:, :, 8, 0, :] = Co.transpose(1, 0, 2).astype(ml_dtypes.bfloat16)
        tab[:, :, 8, 1, :] = Se.transpose(1, 0, 2).astype(ml_dtypes.bfloat16)
        tab[:, :NMETA, 8, 2, :] = cm.reshape(NMETA, NQB, 512).transpose(1, 0, 2).astype(ml_dtypes.bfloat16)
        tab[:, :NMETA, 8, 3, :] = sm.reshape(NMETA, NQB, 512).transpose(1, 0, 2).astype(ml_dtypes.bfloat16)
        del Ce, So
        out[hf] = dict(pos=pos, own=own, other=other, ktab=ktab, qtab=qtab, dft=tab)
    cidx = np.arange(128)
    thc = 2.0 * np.pi * ((cidx[:, None] * cidx[None, :]) % 128) / 128.0
    nrm = 1.0 / np.sqrt(float(L) * 128.0)
    out["cc"] = (np.cos(thc) * nrm).astype(np.float32)
    out["sc"] = (-np.sin(thc) * nrm).astype(np.float32)
    out["ident"] = np.eye(128, dtype=np.float32)
    selm = np.zeros((32, 96), np.float32)
    selm[np.arange(32), 64 + np.arange(32)] = 1.0
    out["sel"] = selm
    _CACHE["tabs"] = out
    return out


def _in_maps(x, meta_tokens, norm_pre_mix, w_in, q_a_norm, w_q_b, kv_a_norm, w_kv_b, w_fourier, b_fourier,
             mix_gain_attn, mix_gain_fourier, w_o, norm_post_mix, norm_pre_ffn, w_gate, w_up, w_down,
             norm_post_ffn):
    T = _host_tables()
    f = lambda a: np.ascontiguousarray(np.asarray(a, dtype=np.float32))
    x = f(x)
    meta = f(meta_tokens)
    shared = {
        "w_in": f(w_in[0]), "w_q_b": f(w_q_b[0]), "w_kv_b": f(w_kv_b[0]), "w_fourier": f(w_fourier[0]),
        "w_o": f(w_o[0]), "w_gate": f(w_gate[0]), "w_up": f(w_up[0]), "w_down": f(w_down[0]),
        "g_pre": f(np.asarray(norm_pre_mix[0]).reshape(8, 128).T),
        "g_q": f(np.asarray(q_a_norm[0]).reshape(2, 128).T),
        "g_kv": f(np.asarray(kv_a_norm[0]).reshape(128, 1)),
        "g_a": f(np.asarray(mix_gain_attn[0]).reshape(4, 128).T),
        "g_f": f(np.asarray(mix_gain_fourier[0]).reshape(4, 128).T),
        "b_f": f(np.asarray(b_fourier[0]).reshape(4, 128).T),
        "g_post": f(norm_post_mix[0]), "g_preffn": f(norm_pre_ffn[0]), "g_postffn": f(norm_post_ffn[0]),
        "cc": T["cc"], "sc": T["sc"], "ident": T["ident"], "sel": T["sel"],
    }
    maps = []
    for core in range(8):
        b, hf = core // 2, core % 2
        t = T[hf]
        own = x[b, t["own"] - NMETA]
        other = x[b, t["other"] - NMETA]
        xperm = np.concatenate([own, other, meta], axis=0)
        m = dict(shared)
        m.update({"xperm": np.ascontiguousarray(xperm), "ktab": t["ktab"], "qtab": t["qtab"], "dft": t["dft"]})
        maps.append(m)
    return maps


def kernel(**inputs):
    if "nc" not in _CACHE:
        _CACHE["nc"] = build_program()
    nc = _CACHE["nc"]
    maps = _in_maps(**inputs)
    res = run_bass_kernel_spmd(nc, maps, core_ids=list(range(8)))
    out = np.empty((4, SEQ, D), np.float32)
    for core in range(8):
        b, hf = core // 2, core % 2
        out[b, hf * NOWN:(hf + 1) * NOWN] = np.asarray(res.results[core]["y"], dtype=np.float32)
    return out
```

```python
import numpy as np
import ml_dtypes
import concourse.bass as bass
import concourse.mybir as mybir
from concourse.bass_utils import run_bass_kernel_spmd

F32 = mybir.dt.float32
BF16 = mybir.dt.bfloat16
AF = mybir.ActivationFunctionType
ALU = mybir.AluOpType

D = 1024
SEQ = 8192
NMETA = 16
L = SEQ + NMETA
NOWN = 4096
NQB = 8
NTB = 17
NLB = 65
H = 8
DFF = 2816
NJ = DFF // 128
EPS = 1e-6
SCALE = 1.0 / float(np.sqrt(96.0))
NCHK = 9

ENGS = ["pe", "act", "dve", "pool", "sp"]


class Prog:
    def __init__(self):
        self.ops = {e: [] for e in ENGS}
        self.last_w = {}
        self.readers = {}
        self.dma_cnt = {}
        self.pending = {e: set() for e in ENGS}

    def op(self, eng, fn, reads=(), writes=(), dma=None):
        idx = len(self.ops[eng])
        deps = set(self.pending[eng])
        self.pending[eng] = set()
        for r in reads:
            w = self.last_w.get(r)
            if w is not None:
                deps.add(w)
        for w_ in writes:
            w = self.last_w.get(w_)
            if w is not None:
                deps.add(w)
            for d in self.readers.get(w_, {}).values():
                deps.add(d)
        if dma is not None:
            cnt = self.dma_cnt.get(dma, 0) + 1
            self.dma_cnt[dma] = cnt
            tok = ("dma", dma, cnt)
            rkey = ("dma", dma)
        else:
            tok = ("eng", eng, idx)
            rkey = ("eng", eng)
        deps.discard(tok)
        self.ops[eng].append(dict(fn=fn, deps=deps, tok=tok, dma=dma))
        for r in reads:
            self.readers.setdefault(r, {})[rkey] = tok
        for w_ in writes:
            self.last_w[w_] = tok
            self.readers[w_] = {}
        return tok

    def barrier(self):
        alld = set()
        for e in ENGS:
            for i in range(len(self.ops[e]) - 1, -1, -1):
                if self.ops[e][i]["dma"] is None:
                    alld.add(self.ops[e][i]["tok"])
                    break
        for k, c in self.dma_cnt.items():
            alld.add(("dma", k, c))
        for e in ENGS:
            self.pending[e] |= alld

    def emit(self, nc, block_engines):
        pub = {e: set() for e in ENGS}
        for e in ENGS:
            for o in self.ops[e]:
                for d in o["deps"]:
                    if d[0] == "eng":
                        if d[1] == e and e == "pe":
                            continue
                        pub[d[1]].add(d[2])
        pubcount = {}
        for e in ENGS:
            c = 0
            m = {}
            for i, o in enumerate(self.ops[e]):
                if o["dma"] is None and i in pub[e]:
                    c += 1
                    m[i] = c
            pubcount[e] = m
        esem = {e: nc.alloc_semaphore("sem_" + e) for e in ENGS}
        dsem = {k: nc.alloc_semaphore("dsem_%d" % i) for i, k in enumerate(self.dma_cnt)}
        final_w = {k: self.dma_cnt.get(k, 0) for k in ("w_sp", "w_pool")}

        def run(e, eng):
            waited = {}
            for i, o in enumerate(self.ops[e]):
                need = {}
                for d in o["deps"]:
                    if d[0] == "eng":
                        if d[1] == e and e == "pe":
                            continue
                        key = ("e", d[1])
                        val = pubcount[d[1]][d[2]]
                    else:
                        key = ("d", d[1])
                        val = 16 * (final_w[d[1]] if d[1] in final_w else d[2])
                    if val > need.get(key, 0):
                        need[key] = val
                for key, val in need.items():
                    if waited.get(key, 0) >= val:
                        continue
                    waited[key] = val
                    sem = esem[key[1]] if key[0] == "e" else dsem[key[1]]
                    eng.wait_ge(sem, val)
                ins = o["fn"](eng)
                if o["dma"] is not None:
                    ins.then_inc(dsem[o["dma"]], 16)
                elif i in pub[e]:
                    ins.then_inc(esem[e], 1)

        for e in ENGS:
            block_engines[e](lambda eng, e=e: run(e, eng))


class Arena:
    def __init__(self, ar, total):
        self.ar = ar
        self.free = [(0, total)]
        self.allocs = {}

    def alloc(self, name, nbytes):
        nbytes = (nbytes + 63) // 64 * 64
        for i, (o, s) in enumerate(self.free):
            if s >= nbytes:
                self.free[i] = (o + nbytes, s - nbytes)
                self.allocs[name] = (o, nbytes)
                return o
        raise MemoryError("SBUF arena full allocating %s (%d B); free=%s" % (name, nbytes, self.free))

    def release(self, *names):
        for name in names:
            o, s = self.allocs.pop(name)
            self.free.append((o, s))
        self.free.sort()
        merged = []
        for o, s in self.free:
            if s == 0:
                continue
            if merged and merged[-1][0] + merged[-1][1] == o:
                merged[-1] = (merged[-1][0], merged[-1][1] + s)
            else:
                merged.append((o, s))
        self.free = merged

    def tile(self, name, shape, dtype):
        esz = 4 if dtype == F32 else 2
        n = 1
        for s in shape[1:]:
            n *= s
        off = self.alloc(name, n * esz)
        v = self.ar[:, off // 2: off // 2 + n * esz // 2]
        if dtype == F32:
            v = v.bitcast(F32)
        if len(shape) > 2:
            names = " ".join("a%d" % i for i in range(len(shape) - 1))
            kw = {"a%d" % i: shape[i + 1] for i in range(len(shape) - 1)}
            v = v.rearrange("p (%s) -> p %s" % (names, names), **kw)
        if shape[0] < 128:
            v = v[0:shape[0]]
        return v


def build_program(stop_after=None):
    nc = bass.Bass("TRN2", target_bir_lowering=False)
    P = Prog()
    _stop = [False]

    def din(name, shape, dt=F32):
        return nc.dram_tensor(name, list(shape), dt, kind="ExternalInput").ap()

    xperm = din("xperm", [L, D])
    w_in = din("w_in", [D, 928])
    w_q_b = din("w_q_b", [256, 768])
    w_kv_b = din("w_kv_b", [128, 1024])
    w_f = din("w_fourier", [4, 128, 128])
    w_o = din("w_o", [D, D])
    w_gate = din("w_gate", [D, DFF])
    w_up = din("w_up", [D, DFF])
    w_down = din("w_down", [DFF, D])
    d_gpre = din("g_pre", [128, 8])
    d_gq = din("g_q", [128, 2])
    d_gkv = din("g_kv", [128, 1])
    d_ga = din("g_a", [128, 4])
    d_gf = din("g_f", [128, 4])
    d_bf = din("b_f", [128, 4])
    d_gpost = din("g_post", [D])
    d_gpreffn = din("g_preffn", [D])
    d_gpostffn = din("g_postffn", [D])
    d_ktab = din("ktab", [64, L])
    d_qtab = din("qtab", [128, NOWN])
    d_cc = din("cc", [128, 128])
    d_sc = din("sc", [128, 128])
    d_ident = din("ident", [128, 128])
    d_sel = din("sel", [32, 96])
    d_dft = din("dft", [NQB, 128, NCHK, 8, 512], BF16)
    y = nc.dram_tensor("y", [NOWN, D], F32, kind="ExternalOutput").ap()
    scr_g = nc.dram_tensor("scr_g", [NJ, 128, 1024], BF16).ap()
    scr_u = nc.dram_tensor("scr_u", [NJ, 128, 1024], BF16).ap()
    scr_d = nc.dram_tensor("scr_d", [NJ, 128, 1024], BF16).ap()
    scr_f = nc.dram_tensor("scr_f", [NQB, 128, 4, 512], BF16).ap()

    TOTAL = 211968
    AR = nc.alloc_sbuf_tensor("arena", [128, TOTAL // 2], BF16)
    A = Arena(AR, TOTAL)
    PS = nc.alloc_psum_tensor("ps", [128, 4096], F32)

    def bank(b, n=1):
        return PS[:, b * 512:(b + n) * 512]

    def bank_bf(b):
        return PS[:, b * 512:(b + 1) * 512].bitcast(BF16)

    def pb(b):
        return ("ps", b)

    ident = A.tile("ident", [128, 128], BF16)
    ones = A.tile("ones", [128, 128], BF16)
    sel = A.tile("sel", [32, 96], BF16)
    epst = A.tile("eps", [128, 1], F32)
    gq = A.tile("gq", [128, 2], F32)
    gkv = A.tile("gkv", [128, 1], F32)
    ga = A.tile("ga", [128, 4], F32)
    gf = A.tile("gf", [128, 4], F32)
    bfb = A.tile("bfb", [128, 4], F32)
    gpre = A.tile("gpre", [128, 8], F32)
    Wq = A.tile("Wq", [128, 2, 8, 128], BF16)
    Wkvb = A.tile("Wkvb", [128, 1024], BF16)
    M12 = A.tile("M12", [128, 8, 128], BF16)

    def dma(q, out, in_, key, reads=(), writes=()):
        if key == "w":
            key = "w_" + q
        return P.op(q, lambda e, out=out, in_=in_: e.dma_start(out=out, in_=in_), reads=reads, writes=writes, dma=key)

    P.op("dve", lambda e: e.memset(ones, 1.0), writes=["ones"])
    P.op("dve", lambda e: e.memset(epst, EPS), writes=["eps"])
    for dst, src, nm in [(gq, d_gq, "gq"), (gkv, d_gkv, "gkv"), (ga, d_ga, "ga"), (gf, d_gf, "gf"),
                         (bfb, d_bf, "bfb"), (gpre, d_gpre, "gpre")]:
        dma("sp", dst, src, "w", writes=[nm])
    dma("pool", ident, d_ident, "w", writes=["ident"])
    dma("pool", sel, d_sel, "w", writes=["sel"])

    Win = A.tile("Win", [128, 8, 928], BF16)
    Wkr = A.tile("Wkr", [128, 8, 64], BF16)
    wst = [A.tile("wst%d" % i, [128, 928], F32) for i in range(2)]
    w_in_r = w_in.rearrange("(c p) n -> p c n", p=128)
    for c in range(8):
        s = c % 2
        dma("sp", wst[s], w_in_r[:, c, :], ("wst", s), writes=[("wst", s)])
        P.op("dve", lambda e, c=c, s=s: e.tensor_scalar(out=Win[:, c, :], in0=wst[s], scalar1=gpre[:, c:c + 1],
                                                       scalar2=None, op0=ALU.mult),
             reads=[("wst", s), "gpre"], writes=["Win"])
        P.op("dve", lambda e, c=c, s=s: e.tensor_scalar(out=Wkr[:, c, 0:32], in0=wst[s][:, 384:416],
                                                       scalar1=gpre[:, c:c + 1], scalar2=None, op0=ALU.mult),
             reads=[("wst", s), "gpre"], writes=["Wkr"])
        P.op("dve", lambda e, c=c, s=s: e.tensor_scalar(out=Wkr[:, c, 32:48], in0=wst[s][:, 400:416],
                                                       scalar1=gpre[:, c:c + 1], scalar2=None, op0=ALU.mult),
             reads=[("wst", s), "gpre"], writes=["Wkr"])
        P.op("dve", lambda e, c=c, s=s: e.tensor_scalar(out=Wkr[:, c, 48:64], in0=wst[s][:, 384:400],
                                                       scalar1=gpre[:, c:c + 1], scalar2=None, op0=ALU.mult),
             reads=[("wst", s), "gpre"], writes=["Wkr"])
    wq_r = w_q_b.rearrange("(c p) (h e) -> p c h e", p=128, e=96)
    for c in range(2):
        dma("pool", Wq[:, c, :, 0:96], wq_r[:, c, :, :], "w", writes=[("Wq", c, 0)])
        dma("pool", Wq[:, c, :, 96:112], wq_r[:, c, :, 80:96], "w", writes=[("Wq", c, 1)])
        dma("pool", Wq[:, c, :, 112:128], wq_r[:, c, :, 64:80], "w", writes=[("Wq", c, 2)])
    dma("pool", Wkvb, w_kv_b, "w", writes=["Wkvb"])
    ccs = A.tile("ccs", [128, 2, 128], BF16)
    wfs = A.tile("wfs", [128, 4, 128], BF16)
    dma("pool", ccs[:, 0, :], d_cc, "w", writes=[("ccs", 0)])
    dma("pool", ccs[:, 1, :], d_sc, "w", writes=[("ccs", 1)])
    dma("pool", wfs, w_f.rearrange("g c d -> c g d"), "w", writes=["wfs"])
    for m in range(2):
        for g in range(4):
            P.op("pe", lambda e, m=m, g=g: e.matmul(bank(m)[:, g * 128:(g + 1) * 128], ccs[:, m, :], wfs[:, g, :],
                                                    start=True, stop=True),
                 reads=[("ccs", m), "wfs"], writes=[pb(m)])
        P.op("dve", lambda e, m=m: e.tensor_copy(out=M12[:, m * 4:(m + 1) * 4, :],
                                                 in_=bank(m).rearrange("p (g d) -> p g d", d=128)),
             reads=[pb(m)], writes=["M12"])
    A.release("ccs", "wfs", "wst0", "wst1")
    P.barrier()

    if stop_after == 'W':
        return _finish(nc, P)
    ckvT = A.tile("ckvT", [128, L], BF16)
    kropeT = A.tile("kropeT", [32, L], BF16)
    cqT = A.tile("cqT", [128, 2, NOWN], BF16)
    f_sb = A.tile("f_sb", [128, NLB, 512], BF16)
    xt = [A.tile("xt%d" % i, [128, D], F32) for i in range(4)]
    hb = [A.tile("hb%d" % i, [128, D], BF16) for i in range(4)]
    hT = [A.tile("hT%d" % i, [128, 8, 512], BF16) for i in range(2)]
    junk = A.tile("junk", [128, D], BF16)
    ssq = A.tile("ssq", [128, 8], F32)
    sq = A.tile("sq", [128, 3, 512], BF16)
    lnb = A.tile("lnb", [128, 2, 512], F32)
    rbc = A.tile("rbc", [128, 2, 512], F32)
    ktb = [A.tile("ktb%d" % i, [64, 512], F32) for i in range(2)]
    krp = A.tile("krp", [64, 512], F32)
    krp2 = A.tile("krp2", [32, 512], F32)
    ftmp = [A.tile("ftmp%d" % i, [128, 512], BF16) for i in range(2)]

    def rstd_col(col, n, nt):
        P.op("act", lambda e: e.activation(out=ssq[:nt, col:col + 1], in_=ssq[:nt, col:col + 1], func=AF.Ln,
                                           scale=1.0 / n, bias=epst[:nt, :]),
             reads=[("ssq", col), "eps"], writes=[("ssq", col)])
        P.op("act", lambda e: e.activation(out=ssq[:nt, col:col + 1], in_=ssq[:nt, col:col + 1], func=AF.Exp,
                                           scale=-0.5),
             reads=[("ssq", col)], writes=[("ssq", col)])

    ptiles = []
    for tb in range(NTB):
        ntok_ = 512 if tb < 16 else 16
        for t in range((ntok_ + 127) // 128):
            ptiles.append((tb, t, min(128, ntok_ - t * 128)))
    NHB = 4

    def p_t1(i):
        tb, t, nt = ptiles[i]
        xs = i % NHB
        bs = i % NHB
        col = i % 4
        r0 = tb * 512 + t * 128
        dma("sp", xt[xs][:nt, :], xperm[r0:r0 + nt, :], ("xt", xs), writes=[("xt", xs)])
        P.op("act", lambda e: e.activation(out=junk[:nt, :], in_=xt[xs][:nt, :], func=AF.Square,
                                           accum_out=ssq[:nt, col:col + 1]),
             reads=[("xt", xs)], writes=["junk", ("ssq", col)])
        rstd_col(col, D, nt)
        P.op("act", lambda e: e.activation(out=hb[bs][:nt, :], in_=xt[xs][:nt, :], func=AF.Identity,
                                           scale=ssq[:nt, col:col + 1]),
             reads=[("xt", xs), ("ssq", col)], writes=[("hb", bs)])

    def p_t2(i):
        tb, t, nt = ptiles[i]
        bs = i % NHB
        hs = tb % 2
        tbk = i % 2
        for c in range(8):
            P.op("pe", lambda e, c=c: e.transpose(
                bank_bf(tbk)[:, c * 128:c * 128 + nt], hb[bs][:nt, c * 128:(c + 1) * 128], ident[:nt, :nt]),
                 reads=[("hb", bs), "ident"], writes=[pb(tbk)])
        P.op("dve", lambda e: e.tensor_copy(
            out=hT[hs][:, :, t * 128:t * 128 + nt],
            in_=bank_bf(tbk).rearrange("p (c n) -> p c n", n=128)[:, :, :nt]),
             reads=[pb(tbk)], writes=[("hT", hs)])

    def p_proj(tb):
        t0 = tb * 512
        ntok = 512 if tb < 16 else 16
        ntile = (ntok + 127) // 128
        hs = tb % 2
        own = tb < NQB
        ks = tb % 2
        if tb + 1 < NTB:
            ntk = 512 if tb + 1 < 16 else 16
            dma("sp", ktb[1 - ks][:, :ntk], d_ktab[:, (tb + 1) * 512:(tb + 1) * 512 + ntk], ("ktb", 1 - ks),
                writes=[("ktb", 1 - ks)])
        for c in range(8):
            P.op("pe", lambda e, c=c, hs=hs, ntok=ntok: e.matmul(bank(2)[:, :ntok], Win[:, c, 256:384],
                                                                hT[hs][:, c, :ntok], start=(c == 0), stop=(c == 7)),
                 reads=["Win", ("hT", hs)], writes=[pb(2)])
        for c in range(8):
            P.op("pe", lambda e, c=c, hs=hs, ntok=ntok: e.matmul(bank(3)[0:64, :ntok], Wkr[:, c, :],
                                                                hT[hs][:, c, :ntok], start=(c == 0), stop=(c == 7)),
                 reads=["Wkr", ("hT", hs)], writes=[pb(3)])
        if own:
            for m in range(2):
                for c in range(8):
                    P.op("pe", lambda e, c=c, m=m, hs=hs: e.matmul(bank(6 + m), Win[:, c, m * 128:(m + 1) * 128],
                                                                  hT[hs][:, c, :], start=(c == 0), stop=(c == 7)),
                         reads=["Win", ("hT", hs)], writes=[pb(6 + m)])
        P.op("act", lambda e, ntok=ntok: e.activation(out=sq[:, 0, :ntok], in_=bank(2)[:, :ntok], func=AF.Square),
             reads=[pb(2)], writes=[("sq", 0)])
        if own:
            for m in range(2):
                P.op("act", lambda e, m=m: e.activation(out=sq[:, 1 + m, :], in_=bank(6 + m), func=AF.Square),
                     reads=[pb(6 + m)], writes=[("sq", 1 + m)])
        P.op("pe", lambda e, ntok=ntok: e.matmul(bank(4)[:, :ntok], ones, sq[:, 0, :ntok], start=True, stop=True),
             reads=["ones", ("sq", 0)], writes=[pb(4)])
        if own:
            for m in range(2):
                P.op("pe", lambda e, m=m: e.matmul(bank(5), ones, sq[:, 1 + m, :], start=(m == 0), stop=(m == 1)),
                     reads=["ones", ("sq", 1 + m)], writes=[pb(5)])
        P.op("act", lambda e, ntok=ntok: e.activation(out=lnb[:, 0, :ntok], in_=bank(4)[:, :ntok], func=AF.Ln,
                                                      scale=1.0 / 128, bias=epst),
             reads=[pb(4), "eps"], writes=[("lnb", 0)])
        P.op("act", lambda e, ntok=ntok: e.activation(out=rbc[:, 0, :ntok], in_=lnb[:, 0, :ntok], func=AF.Exp,
                                                      scale=-0.5),
             reads=[("lnb", 0)], writes=[("rbc", 0)])
        if own:
            P.op("act", lambda e: e.activation(out=lnb[:, 1, :], in_=bank(5), func=AF.Ln, scale=1.0 / 256, bias=epst),
                 reads=[pb(5), "eps"], writes=[("lnb", 1)])
            P.op("act", lambda e: e.activation(out=rbc[:, 1, :], in_=lnb[:, 1, :], func=AF.Exp, scale=-0.5),
                 reads=[("lnb", 1)], writes=[("rbc", 1)])
        P.op("dve", lambda e, t0=t0, ntok=ntok: e.scalar_tensor_tensor(
            out=ckvT[:, t0:t0 + ntok], in0=bank(2)[:, :ntok], scalar=gkv[:, 0:1], in1=rbc[:, 0, :ntok],
            op0=ALU.mult, op1=ALU.mult),
             reads=[pb(2), "gkv", ("rbc", 0)], writes=[("ckvT", tb)])
        if own:
            for m in range(2):
                P.op("dve", lambda e, m=m, t0=t0: e.scalar_tensor_tensor(
                    out=cqT[:, m, t0:t0 + 512], in0=bank(6 + m), scalar=gq[:, m:m + 1], in1=rbc[:, 1, :],
                    op0=ALU.mult, op1=ALU.mult),
                     reads=[pb(6 + m), "gq", ("rbc", 1)], writes=[("cqT", tb)])
        P.op("dve", lambda e, ks=ks, ntok=ntok: e.tensor_tensor(out=krp[0:32, :ntok], in0=bank(3)[0:32, :ntok],
                                                               in1=ktb[ks][0:32, :ntok], op=ALU.mult),
             reads=[pb(3), ("ktb", ks)], writes=["krp"])
        P.op("dve", lambda e, ks=ks, ntok=ntok: e.tensor_tensor(out=krp2[0:32, :ntok], in0=bank(3)[32:64, :ntok],
                                                               in1=ktb[ks][32:64, :ntok], op=ALU.mult),
             reads=[pb(3), ("ktb", ks)], writes=["krp2"])
        P.op("dve", lambda e, t0=t0, ntok=ntok: e.tensor_tensor(out=kropeT[:, t0:t0 + ntok], in0=krp[0:32, :ntok],
                                                               in1=krp2[0:32, :ntok], op=ALU.add),
             reads=["krp", "krp2"], writes=[("kropeT", tb)])
        for t in range(ntile):
            nt = min(128, ntok - t * 128)
            fb = t % 2
            lb = tb * 4 + t
            for c in range(8):
                P.op("pe", lambda e, c=c, hs=hs, t=t, nt=nt, fb=fb: e.matmul(
                    bank(fb)[:nt, :], hT[hs][:, c, t * 128:t * 128 + nt], Win[:, c, 416:928],
                    start=(c == 0), stop=(c == 7)),
                     reads=["Win", ("hT", hs)], writes=[pb(fb)])
            P.op("dve", lambda e, nt=nt, fb=fb, lb=lb: e.tensor_copy(out=f_sb[:nt, lb, :], in_=bank(fb)[:nt, :]),
                 reads=[pb(fb)], writes=[("f", lb)])
            if 32 <= lb < 64:
                b = lb - 32
                fs_ = b % 2
                P.op("dve", lambda e, b=b, fs_=fs_: e.tensor_tensor(out=ftmp[fs_], in0=f_sb[:, b, :],
                                                                  in1=f_sb[:, b + 32, :], op=ALU.subtract),
                     reads=[("f", b), ("f", b + 32)], writes=[("ftmp", fs_)])
                P.op("dve", lambda e, b=b: e.tensor_tensor(out=f_sb[:, b, :], in0=f_sb[:, b, :],
                                                           in1=f_sb[:, b + 32, :], op=ALU.add),
                     reads=[("f", b), ("f", b + 32)], writes=[("f", b)])
                P.op("pool", lambda e, b=b, fs_=fs_: e.tensor_copy(out=f_sb[:, b + 32, :], in_=ftmp[fs_]),
                     reads=[("ftmp", fs_)], writes=[("f", b + 32)])

    dma("sp", ktb[0], d_ktab[:, 0:512], ("ktb", 0), writes=[("ktb", 0)])
    p_t1(0)
    p_t1(1)
    for i in range(len(ptiles)):
        if i + 2 < len(ptiles):
            p_t1(i + 2)
        p_t2(i)
        if i + 1 == len(ptiles) or ptiles[i + 1][0] != ptiles[i][0]:
            p_proj(ptiles[i][0])
    A.release("Win", "Wkr", "xt0", "xt1", "xt2", "xt3", "hb0", "hb1", "hb2", "hb3", "hT0", "hT1", "junk", "sq", "lnb", "rbc",
              "ktb0", "ktb1", "krp", "krp2", "ftmp0", "ftmp1")
    P.barrier()

    if stop_after == 'P':
        return _finish(nc, P)
    fNb = [A.tile("fNb%d" % i, [128, 4, 512], BF16) for i in range(2)]
    tabs = [A.tile("tab%d" % i, [128, 8, 512], BF16) for i in range(3)]
    ABs = A.tile("ABs", [128, 8, 512], BF16)
    ysb = A.tile("ysb", [128, 4, 512], F32)
    ysq = A.tile("ysq", [128, 4, 512], BF16)
    lnf = A.tile("lnf", [128, 512], F32)
    rbf = A.tile("rbf", [128, 512], F32)
    chunk_state = [0]

    def f_load(kb, ch):
        ts_ = chunk_state[0] % 3
        chunk_state[0] += 1
        dma("sp", tabs[ts_], d_dft[kb, :, ch, :, :], ("tab", ts_), writes=[("tab", ts_)])
        return ts_

    def f_mm_a(ch, ts_):
        for j in range(4):
            b = ch * 4 + j
            for g in range(4):
                P.op("pe", lambda e, j=j, b=b, g=g: e.matmul(
                    bank(g), f_sb[:, b, g * 128:(g + 1) * 128], tabs[ts_][:, 2 * j, :],
                    start=(b == 0), stop=False),
                     reads=[("f", b), ("tab", ts_)], writes=[pb(g)])

    def f_mm_b(ch, ts_, groups):
        for j in range(4):
            b = ch * 4 + j
            for g in groups:
                P.op("pe", lambda e, j=j, b=b, g=g: e.matmul(
                    bank(4 + g), f_sb[:, 32 + b, g * 128:(g + 1) * 128], tabs[ts_][:, 2 * j + 1, :],
                    start=(b == 0), stop=False),
                     reads=[("f", 32 + b), ("tab", ts_)], writes=[pb(4 + g)])

    def f_mm_edge(ts_):
        for g in range(4):
            P.op("pe", lambda e, g=g: e.matmul(
                bank(g), f_sb[:, 63, g * 128:(g + 1) * 128], tabs[ts_][:, 0, :], start=False, stop=False),
                 reads=[("f", 63), ("tab", ts_)], writes=[pb(g)])
            P.op("pe", lambda e, g=g: e.matmul(
                bank(4 + g), f_sb[:, 31, g * 128:(g + 1) * 128], tabs[ts_][:, 1, :], start=False, stop=False),
                 reads=[("f", 31), ("tab", ts_)], writes=[pb(4 + g)])
            P.op("pe", lambda e, g=g: e.matmul(
                bank(g), f_sb[:16, 64, g * 128:(g + 1) * 128], tabs[ts_][:16, 2, :], start=False, stop=True),
                 reads=[("f", 64), ("tab", ts_)], writes=[pb(g)])
            P.op("pe", lambda e, g=g: e.matmul(
                bank(4 + g), f_sb[:16, 64, g * 128:(g + 1) * 128], tabs[ts_][:16, 3, :], start=False, stop=True),
                 reads=[("f", 64), ("tab", ts_)], writes=[pb(4 + g)])

    def f_tail_evac():
        for g in range(4):
            P.op("act", lambda e, g=g: e.copy(out=ABs[:, g, :], in_=bank(g)), reads=[pb(g)], writes=[("ABs", g)])
            P.op("dve", lambda e, g=g: e.tensor_copy(out=ABs[:, 4 + g, :], in_=bank(4 + g)), reads=[pb(4 + g)],
                 writes=[("ABs", 4 + g)])

    def f_tail_y():
        for g in range(4):
            P.op("pe", lambda e, g=g: e.matmul(bank(4 + g), M12[:, g, :], ABs[:, g, :], start=True, stop=False),
                 reads=["M12", ("ABs", g)], writes=[pb(4 + g)])
            P.op("pe", lambda e, g=g: e.matmul(bank(4 + g), M12[:, 4 + g, :], ABs[:, 4 + g, :], start=False,
                                               stop=True),
                 reads=["M12", ("ABs", 4 + g)], writes=[pb(4 + g)])
            P.op("act", lambda e, g=g: e.activation(out=ysb[:, g, :], in_=bank(4 + g), func=AF.Identity,
                                                    bias=bfb[:, g:g + 1]),
                 reads=[pb(4 + g), "bfb"], writes=[("ysb", g)])
            P.op("dve", lambda e, g=g: e.tensor_tensor(out=ysq[:, g, :], in0=ysb[:, g, :], in1=ysb[:, g, :],
                                                       op=ALU.mult),
                 reads=[("ysb", g)], writes=[("ysq", g)])

    def f_tail_stats(kb):
        for g in range(4):
            P.op("pe", lambda e, g=g: e.matmul(bank(7), ones, ysq[:, g, :], start=(g == 0), stop=(g == 3)),
                 reads=["ones", ("ysq", g)], writes=[pb(7)])
        P.op("act", lambda e: e.activation(out=lnf, in_=bank(7), func=AF.Ln, scale=1.0 / 512, bias=epst),
             reads=[pb(7), "eps"], writes=["lnf"])
        P.op("act", lambda e: e.activation(out=rbf, in_=lnf, func=AF.Exp, scale=-0.5), reads=["lnf"], writes=["rbf"])
        fs = kb % 2
        for g in range(4):
            P.op("dve", lambda e, g=g: e.scalar_tensor_tensor(
                out=fNb[fs][:, g, :], in0=ysb[:, g, :], scalar=gf[:, g:g + 1], in1=rbf,
                op0=ALU.mult, op1=ALU.mult),
                 reads=[("ysb", g), "gf", "rbf"], writes=[("fNb", fs)])
        dma("pool", scr_f[kb], fNb[fs], ("fNout", fs), reads=[("fNb", fs)], writes=["scr_f"])

    for kb in range(NQB):
        ts0 = f_load(kb, 0)
        if kb == 0:
            f_mm_a(0, ts0)
            f_mm_b(0, ts0, [0, 1, 2, 3])
        else:
            f_mm_a(0, ts0)
            f_tail_y()
            f_mm_b(0, ts0, [0, 1, 2])
            f_tail_stats(kb - 1)
            f_mm_b(0, ts0, [3])
        for ch in range(1, 8):
            ts_ = f_load(kb, ch)
            f_mm_a(ch, ts_)
            f_mm_b(ch, ts_, [0, 1, 2, 3])
        ts_ = f_load(kb, 8)
        f_mm_edge(ts_)
        f_tail_evac()
    f_tail_y()
    f_tail_stats(NQB - 1)
    A.release("f_sb", "tab0", "tab1", "tab2", "ABs", "ysb", "ysq", "lnf", "rbf", "fNb0", "fNb1")
    P.barrier()

    if stop_after == 'F':
        return _finish(nc, P)
    attnT = A.tile("attnT", [128, 4, NOWN], BF16)
    KT = [A.tile("KT%d" % i, [96, L], BF16) for i in range(2)]
    VA = [A.tile("VA%d" % i, [128, NLB, 128], BF16) for i in range(2)]
    QT = [A.tile("QT%d" % i, [96, 512], BF16) for i in range(2)]
    PT = [A.tile("PT%d" % i, [128, 1024], BF16) for i in range(3)]
    qtab = A.tile("qtab", [128, NOWN], F32)
    tq = A.tile("tq", [128, 2, 512], F32)
    rcp = A.tile("rcp", [64, 512], F32)
    stg = [A.tile("stg%d" % i, [128, 1024], BF16) for i in range(4)]
    dma("sp", qtab, d_qtab, "qtab", writes=["qtab"])
    for i in range(2):
        P.op("pool", lambda e, i=i: e.memset(VA[i][:, :, 64:128], 1.0), writes=[("VAones", i)])

    wg_r = w_gate.rearrange("(c p) (j n) -> p j c n", p=128, n=128)
    wu_r = w_up.rearrange("(c p) (j n) -> p j c n", p=128, n=128)
    wd_r = w_down.rearrange("(j p) n -> p j n", p=128)
    sctr = 0
    for j in range(NJ):
        for src, dst, is3 in [(wg_r, scr_g, True), (wu_r, scr_u, True), (wd_r, scr_d, False)]:
            s = sctr % 4
            sctr += 1
            if is3:
                dma("pool", stg[s].rearrange("p (c n) -> p c n", n=128), src[:, j, :, :], ("stgin", s),
                    writes=[("stg", s)])
            else:
                dma("pool", stg[s], src[:, j, :], ("stgin", s), writes=[("stg", s)])
            dma("sp", dst[j], stg[s], ("stgout", s), reads=[("stg", s)], writes=["scr"])

    osb = A.tile("osb", [128, 512], F32)
    JB = 7
    OB = 6

    def kv_steps(h, banks=(7,), use_act=False):
        kv = h % 2
        steps = []
        bctr = [0]
        for tb in range(NTB):
            def st(tb=tb):
                t0 = tb * 512
                ntok = 512 if tb < 16 else 16
                JB = banks[bctr[0] % len(banks)]
                bctr[0] += 1
                P.op("pe", lambda e: e.matmul(bank(JB)[0:64, :ntok], Wkvb[:, h * 128:h * 128 + 64],
                                              ckvT[:, t0:t0 + ntok], start=True, stop=True),
                     reads=["Wkvb", ("ckvT", tb)], writes=[pb(JB)])
                if use_act and tb % 2 == 1:
                    P.op("act", lambda e: e.copy(out=KT[kv][0:64, t0:t0 + ntok], in_=bank(JB)[0:64, :ntok]),
                         reads=[pb(JB)], writes=[("KT", kv)])
                else:
                    P.op("dve", lambda e: e.tensor_copy(out=KT[kv][0:64, t0:t0 + ntok], in_=bank(JB)[0:64, :ntok]),
                         reads=[pb(JB)], writes=[("KT", kv)])
                if h < 2:
                    P.op("dve", lambda e: e.tensor_copy(out=KT[kv][64:96, t0:t0 + ntok],
                                                        in_=kropeT[:, t0:t0 + ntok]),
                         reads=[("kropeT", tb)], writes=[("KT", kv)])
            steps.append(st)
        for l0 in range(0, NLB, 8):
            def st(l0=l0):
                nl = min(8, NLB - l0)
                JB = banks[bctr[0] % len(banks)]
                bctr[0] += 1
                for li in range(nl):
                    lb = l0 + li
                    kp = 128 if lb < 64 else 16
                    P.op("pe", lambda e, lb=lb, li=li, kp=kp: e.matmul(
                        bank(JB)[:kp, li * 64:(li + 1) * 64], ckvT[:, lb * 128:lb * 128 + kp],
                        Wkvb[:, h * 128 + 64:h * 128 + 128], start=True, stop=True),
                         reads=["Wkvb"] + [("ckvT", lb // 4)], writes=[pb(JB)])
                if l0 + nl <= 64 and use_act and (l0 // 8) % 2 == 0:
                    P.op("act", lambda e: e.copy(
                        out=VA[kv][:, l0:l0 + nl, 0:64],
                        in_=bank(JB).rearrange("p (l d) -> p l d", d=64)[:, 0:nl, :]),
                         reads=[pb(JB)], writes=[("VA", kv)])
                elif l0 + nl <= 64:
                    P.op("dve", lambda e: e.tensor_copy(
                        out=VA[kv][:, l0:l0 + nl, 0:64],
                        in_=bank(JB).rearrange("p (l d) -> p l d", d=64)[:, 0:nl, :]),
                         reads=[pb(JB)], writes=[("VA", kv)])
                else:
                    P.op("dve", lambda e: e.tensor_copy(out=VA[kv][0:16, l0, 0:64], in_=bank(JB)[0:16, 0:64]),
                         reads=[pb(JB)], writes=[("VA", kv)])
            steps.append(st)
        return steps

    def q_build(h, qb, qs):
        q0 = qb * 512
        for c in range(2):
            P.op("pe", lambda e, c=c: e.matmul(bank(JB), Wq[:, c, h, :], cqT[:, c, q0:q0 + 512],
                                               start=(c == 0), stop=(c == 1)),
                 reads=[("Wq", c, 0), ("Wq", c, 1), ("Wq", c, 2), ("cqT", qb)], writes=[pb(JB)])
        P.op("dve", lambda e: e.tensor_copy(out=QT[qs][0:64, :], in_=bank(JB)[0:64, :]),
             reads=[pb(JB)], writes=[("QT", qs)])
        P.op("dve", lambda e: e.tensor_tensor(out=tq[64:96, 0, :], in0=bank(JB)[64:96, :],
                                              in1=qtab[64:96, q0:q0 + 512], op=ALU.mult),
             reads=[pb(JB), "qtab"], writes=["tq0"])
        P.op("dve", lambda e: e.tensor_tensor(out=tq[64:96, 1, :], in0=bank(JB)[96:128, :],
                                              in1=qtab[96:128, q0:q0 + 512], op=ALU.mult),
             reads=[pb(JB), "qtab"], writes=["tq1"])
        P.op("dve", lambda e: e.tensor_tensor(out=QT[qs][64:96, :], in0=tq[64:96, 0, :], in1=tq[64:96, 1, :],
                                              op=ALU.add),
             reads=["tq0", "tq1"], writes=[("QT", qs)])

    items = [(h, qb) for h in range(H) for qb in range(NQB)]
    for st in kv_steps(0, banks=(7, 6, 5, 4), use_act=True):
        st()
    q_build(0, 0, 0)
    pending = []
    def attn_item(it, h, qb, pending):
        kv = h % 2
        q0 = qb * 512
        qs = it % 2

        def mm_s(kg):
            g = kg % 3
            for i in range(2):
                lb = kg * 2 + i
                P.op("pe", lambda e, lb=lb, i=i, g=g: e.matmul(
                    bank(g * 2 + i), KT[kv][:, lb * 128:(lb + 1) * 128], QT[qs], start=True, stop=True),
                     reads=[("KT", kv), ("QT", qs)], writes=[pb(g * 2 + i)])

        mm_s(0)
        mm_s(1)
        for kg in range(32):
            g = kg % 3
            ps_ = kg % 3
            if kg + 2 < 32:
                mm_s(kg + 2)
            elif kg + 2 == 32:
                gm = 32 % 3
                P.op("pe", lambda e, gm=gm: e.matmul(bank(gm * 2)[0:16, :], KT[kv][:, 8192:8208], QT[qs],
                                                     start=True, stop=True),
                     reads=[("KT", kv), ("QT", qs)], writes=[pb(gm * 2)])
            P.op("act", lambda e, g=g, ps_=ps_: e.activation(out=PT[ps_], in_=bank(g * 2, 2), func=AF.Exp,
                                                            scale=SCALE),
                 reads=[pb(g * 2), pb(g * 2 + 1)], writes=[("PT", ps_)])
            for i in range(2):
                lb = kg * 2 + i
                P.op("pe", lambda e, lb=lb, i=i, ps_=ps_: e.matmul(
                    bank(OB), VA[kv][:, lb, :], PT[ps_][:, i * 512:(i + 1) * 512], start=(lb == 0), stop=False),
                     reads=[("VA", kv), ("VAones", kv), ("PT", ps_)], writes=[pb(OB)])
            if kg == 6 and it + 1 < len(items):
                q_build(items[it + 1][0], items[it + 1][1], (it + 1) % 2)
            if kg >= 8 and kg % 4 == 0 and pending and qb >= 1:
                pending.pop(0)()
        gm = 32 % 3
        pm = 32 % 3
        P.op("act", lambda e, gm=gm, pm=pm: e.activation(out=PT[pm][0:16, 0:512], in_=bank(gm * 2)[0:16, :],
                                                        func=AF.Exp, scale=SCALE),
             reads=[pb(gm * 2)], writes=[("PT", pm)])
        P.op("pe", lambda e, pm=pm: e.matmul(bank(OB), VA[kv][0:16, 64, :], PT[pm][0:16, 0:512],
                                             start=False, stop=True),
             reads=[("VA", kv), ("VAones", kv), ("PT", pm)], writes=[pb(OB)])
        P.op("dve", lambda e: e.tensor_copy(out=osb, in_=bank(OB)), reads=[pb(OB)], writes=["osb"])
        P.op("dve", lambda e: e.reciprocal(out=rcp, in_=osb[64:128, :]), reads=["osb"], writes=["rcp"])
        po = (h % 2) * 64
        P.op("dve", lambda e, h=h, po=po, q0=q0: e.tensor_tensor(
            out=attnT[po:po + 64, h // 2, q0:q0 + 512], in0=osb[0:64, :], in1=rcp, op=ALU.mult),
             reads=["osb", "rcp"], writes=["attnT"])
        if qb == NQB - 1:
            while pending:
                pending.pop(0)()
    for it, (h, qb) in enumerate(items):
        if qb == 0 and h + 1 < H:
            pending = kv_steps(h + 1)
        attn_item(it, h, qb, pending)
    A.release("KT0", "KT1", "VA0", "VA1", "QT0", "QT1", "PT0", "PT1", "PT2", "qtab", "tq", "rcp", "osb",
              "stg0", "stg1", "stg2", "stg3", "ckvT", "kropeT", "cqT")
    P.barrier()

    if stop_after == 'A':
        return _finish(nc, P)
    hfT = A.tile("hfT", [128, 8, NOWN], BF16)
    Wo = A.tile("Wo", [128, 8, D], BF16)
    gpost = A.tile("gpost", [128, D], F32)
    gpreffn = A.tile("gpreffn", [128, D], F32)
    mT = [A.tile("mT%d" % i, [128, 4, 512], BF16) for i in range(2)]
    fN = [A.tile("fN%d" % i, [128, 4, 512], BF16) for i in range(2)]
    asq = A.tile("asq", [128, 4, 512], BF16)
    lna = A.tile("lna", [128, 512], F32)
    rba = A.tile("rba", [128, 512], F32)
    NOB = 4
    xo = [A.tile("xo%d" % i, [128, D], F32) for i in range(NOB)]
    r1 = [A.tile("r1%d" % i, [128, D], F32) for i in range(NOB)]
    hfb = [A.tile("hfb%d" % i, [128, D], BF16) for i in range(NOB)]
    junk2 = A.tile("junk2", [128, D], BF16)
    junk2b = A.tile("junk2b", [128, D], BF16)
    ssq2 = A.tile("ssq2", [128, 8], F32)
    dma("pool", Wo, w_o.rearrange("(c p) n -> p c n", p=128), "wo", writes=["Wo"])
    dma("sp", gpost, d_gpost.partition_broadcast(128), "gp_a", writes=["gpost"])
    dma("sp", gpreffn, d_gpreffn.partition_broadcast(128), "gp_b", writes=["gpreffn"])

    def rstd2(col, n):
        P.op("act", lambda e: e.activation(out=ssq2[:, col:col + 1], in_=ssq2[:, col:col + 1], func=AF.Ln,
                                           scale=1.0 / n, bias=epst),
             reads=[("ssq2", col), "eps"], writes=[("ssq2", col)])
        P.op("act", lambda e: e.activation(out=ssq2[:, col:col + 1], in_=ssq2[:, col:col + 1], func=AF.Exp,
                                           scale=-0.5),
             reads=[("ssq2", col)], writes=[("ssq2", col)])

    def o_prefix(qb):
        q0 = qb * 512
        fs = qb % 2
        dma("sp", fN[fs], scr_f[qb], ("fNin", fs), reads=["scr_f"], writes=[("fN", fs)])
        for pr in range(4):
            P.op("act", lambda e, pr=pr: e.activation(out=asq[:, pr, :], in_=attnT[:, pr, q0:q0 + 512],
                                                      func=AF.Square),
                 reads=["attnT"], writes=[("asq", pr)])
            P.op("pe", lambda e, pr=pr: e.matmul(bank(6), ones, asq[:, pr, :], start=(pr == 0), stop=(pr == 3)),
                 reads=["ones", ("asq", pr)], writes=[pb(6)])
        P.op("act", lambda e: e.activation(out=lna, in_=bank(6), func=AF.Ln, scale=1.0 / 512, bias=epst),
             reads=[pb(6), "eps"], writes=["lna"])
        P.op("act", lambda e: e.activation(out=rba, in_=lna, func=AF.Exp, scale=-0.5), reads=["lna"], writes=["rba"])
        for pr in range(4):
            P.op("dve", lambda e, pr=pr: e.scalar_tensor_tensor(
                out=mT[fs][:, pr, :], in0=attnT[:, pr, q0:q0 + 512], scalar=ga[:, pr:pr + 1], in1=rba,
                op0=ALU.mult, op1=ALU.mult),
                 reads=["attnT", "ga", "rba"], writes=[("mT", fs, pr)])

    def o_tile_a(qb, t, tctr):
        q0 = qb * 512
        fs = qb % 2
        s = tctr % NOB
        pbk = (tctr % 2) * 2
        c0 = (tctr % NOB) * 2
        r0 = q0 + t * 128
        if tctr + 3 < len(otiles):
            qn, tn = otiles[tctr + 3]
            rn = qn * 512 + tn * 128
            sn = (tctr + 3) % NOB
            dma("sp", xo[sn], xperm[rn:rn + 128, :], ("xo", sn), writes=[("xo", sn)])
        for hf in range(2):
            bk = pbk + hf
            for c in range(8):
                lhs = (lambda c=c: mT[fs][:, c, t * 128:(t + 1) * 128]) if c < 4 else \
                    (lambda c=c: fN[fs][:, c - 4, t * 128:(t + 1) * 128])
                P.op("pe", lambda e, lhs=lhs, c=c, hf=hf, bk=bk: e.matmul(
                    bank(bk), lhs(), Wo[:, c, hf * 512:(hf + 1) * 512], start=(c == 0), stop=(c == 7)),
                     reads=["Wo", ("fN", fs)] + [("mT", fs, c) for c in range(4)], writes=[pb(bk)])
        P.op("act", lambda e: e.activation(out=junk2, in_=bank(pbk, 2), func=AF.Square,
                                           accum_out=ssq2[:, c0:c0 + 1]),
             reads=[pb(pbk), pb(pbk + 1)], writes=["junk2", ("ssq2", c0)])
        rstd2(c0, D)
        P.op("dve", lambda e: e.scalar_tensor_tensor(
            out=r1[s], in0=bank(pbk, 2), scalar=ssq2[:, c0:c0 + 1], in1=gpost, op0=ALU.mult, op1=ALU.mult),
             reads=[pb(pbk), pb(pbk + 1), ("ssq2", c0), "gpost"], writes=[("r1", s)])
        P.op("dve", lambda e: e.tensor_tensor(out=r1[s], in0=r1[s], in1=xo[s], op=ALU.add),
             reads=[("r1", s), ("xo", s)], writes=[("r1", s)])
        dma("pool", y[r0:r0 + 128, :], r1[s], ("h1out", s), reads=[("r1", s)], writes=[("yrow", r0)])

    def o_tile_b(qb, t, tctr):
        q0 = qb * 512
        s = tctr % NOB
        c0 = (tctr % NOB) * 2
        r0 = q0 + t * 128
        P.op("act", lambda e: e.activation(out=junk2b, in_=r1[s], func=AF.Square,
                                           accum_out=ssq2[:, c0 + 1:c0 + 2]),
             reads=[("r1", s)], writes=["junk2b", ("ssq2", c0 + 1)])
        rstd2(c0 + 1, D)
        P.op("dve", lambda e: e.scalar_tensor_tensor(
            out=hfb[s], in0=r1[s], scalar=ssq2[:, c0 + 1:c0 + 2], in1=gpreffn, op0=ALU.mult, op1=ALU.mult),
             reads=[("r1", s), ("ssq2", c0 + 1), "gpreffn"], writes=[("hfb", s)])

    def o_tile_b2(qb, t, tctr):
        q0 = qb * 512
        s = tctr % NOB
        r0 = q0 + t * 128
        tbk = 4 + (tctr % 2)
        for c in range(8):
            P.op("pe", lambda e, c=c: e.transpose(bank_bf(tbk)[:, c * 128:(c + 1) * 128],
                                                  hfb[s][:, c * 128:(c + 1) * 128], ident),
                 reads=[("hfb", s), "ident"], writes=[pb(tbk)])
        P.op("dve", lambda e: e.tensor_copy(
            out=hfT[:, :, r0:r0 + 128], in_=bank_bf(tbk).rearrange("p (c n) -> p c n", n=128)),
             reads=[pb(tbk)], writes=[("hfT", qb)])

    otiles = [(qb, t) for qb in range(NQB) for t in range(4)]
    for i0 in range(3):
        dma("sp", xo[i0], xperm[i0 * 128:(i0 + 1) * 128, :], ("xo", i0), writes=[("xo", i0)])
    o_prefix(0)
    o_tile_a(otiles[0][0], otiles[0][1], 0)
    for i in range(len(otiles) + 1):
        if i < len(otiles):
            qb, t = otiles[i]
            if t == 0 and qb + 1 < NQB:
                o_prefix(qb + 1)
        if i + 1 < len(otiles):
            o_tile_a(otiles[i + 1][0], otiles[i + 1][1], i + 1)
        if i < len(otiles):
            o_tile_b(otiles[i][0], otiles[i][1], i)
        if i >= 1:
            o_tile_b2(otiles[i - 1][0], otiles[i - 1][1], i - 1)
    A.release("Wo", "gpost", "gpreffn", "mT0", "mT1", "asq", "lna", "rba", "xo0", "xo1", "xo2", "xo3", "r10", "r11",
              "r12", "r13", "hfb0", "hfb1", "hfb2", "hfb3",
              "junk2", "junk2b", "attnT", "fN0", "fN1")
    P.barrier()

    if stop_after == 'O':
        return _finish(nc, P)
    Wd = A.tile("Wd", [128, NJ, D], BF16)
    act_sb = A.tile("act_sb", [128, NJ, 512], BF16)
    wgu = [A.tile("wgu%d" % i, [128, 2, 8, 128], BF16) for i in range(3)]
    sg = [A.tile("sg%d" % i, [128, 512], F32) for i in range(2)]
    gpostffn = A.tile("gpostffn", [128, D], F32)
    h1 = [A.tile("h1%d" % i, [128, D], F32) for i in range(2)]
    fo = [A.tile("fo%d" % i, [128, D], F32) for i in range(2)]
    junk3 = A.tile("junk3", [128, D], BF16)
    dma("sp", gpostffn, d_gpostffn.partition_broadcast(128), "gp2", writes=["gpostffn"])
    wctr = 0
    tctr = 0
    for qb in range(NQB):
        q0 = qb * 512
        for j in range(NJ):
            ws = wctr % 3
            gb = (wctr % 2) * 2
            ss = wctr % 2
            wctr += 1
            dma("sp", wgu[ws][:, 0, :, :].rearrange("p c n -> p (c n)"), scr_g[j], ("wgu", ws), reads=["scr"],
                writes=[("wgu", ws)])
            dma("sp", wgu[ws][:, 1, :, :].rearrange("p c n -> p (c n)"), scr_u[j], ("wgu", ws), reads=["scr"],
                writes=[("wgu", ws)])
            if qb == 0:
                dma("pool", Wd[:, j, :], scr_d[j], ("wd", j), reads=["scr"], writes=[("Wd", j)])
            for gu in range(2):
                for c in range(8):
                    P.op("pe", lambda e, gu=gu, c=c, ws=ws, gb=gb, q0=q0: e.matmul(
                        bank(gb + gu), wgu[ws][:, gu, c, :], hfT[:, c, q0:q0 + 512], start=(c == 0), stop=(c == 7)),
                         reads=[("wgu", ws), ("hfT", qb)], writes=[pb(gb + gu)])
            P.op("act", lambda e, gb=gb, ss=ss: e.activation(out=sg[ss], in_=bank(gb), func=AF.Silu),
                 reads=[pb(gb)], writes=[("sg", ss)])
            P.op("dve", lambda e, gb=gb, ss=ss, j=j: e.tensor_tensor(out=act_sb[:, j, :], in0=sg[ss],
                                                                    in1=bank(gb + 1), op=ALU.mult),
                 reads=[("sg", ss), pb(gb + 1)], writes=[("act", j)])
        for t in range(4):
            s = tctr % 2
            tctr += 1
            r0 = q0 + t * 128
            dma("sp", h1[s], y[r0:r0 + 128, :], ("h1in", s), reads=[("yrow", r0)], writes=[("h1", s)])
            for hf in range(2):
                bk = 4 + 2 * s + hf
                for j in range(NJ):
                    P.op("pe", lambda e, j=j, t=t, hf=hf, bk=bk: e.matmul(
                        bank(bk), act_sb[:, j, t * 128:(t + 1) * 128], Wd[:, j, hf * 512:(hf + 1) * 512],
                        start=(j == 0), stop=(j == NJ - 1)),
                         reads=[("Wd", j), ("act", j)], writes=[pb(bk)])
            bk0 = 4 + 2 * s
            col = 4 + s
            P.op("act", lambda e, bk0=bk0, col=col: e.activation(out=junk3, in_=bank(bk0, 2), func=AF.Square,
                                                                accum_out=ssq2[:, col:col + 1]),
                 reads=[pb(bk0), pb(bk0 + 1)], writes=["junk3", ("ssq2", col)])
            rstd2(col, D)
            P.op("dve", lambda e, s=s, bk0=bk0, col=col: e.scalar_tensor_tensor(
                out=fo[s], in0=bank(bk0, 2), scalar=ssq2[:, col:col + 1], in1=gpostffn, op0=ALU.mult, op1=ALU.mult),
                 reads=[pb(bk0), pb(bk0 + 1), ("ssq2", col), "gpostffn"], writes=[("fo", s)])
            P.op("dve", lambda e, s=s: e.tensor_tensor(out=fo[s], in0=fo[s], in1=h1[s], op=ALU.add),
                 reads=[("fo", s), ("h1", s)], writes=[("fo", s)])
            dma("pool", y[r0:r0 + 128, :], fo[s], ("yout", s), reads=[("fo", s), ("h1", s)], writes=[("yrow", r0)])
    return _finish(nc, P)


def _finish(nc, P):
    P.barrier()
    for e in ENGS:
        P.op(e, lambda eng: eng.nop(), reads=(), writes=())
    with nc.Block() as block:
        P.emit(nc, {"pe": block.tensor, "act": block.scalar, "dve": block.vector, "pool": block.gpsimd,
                    "sp": block.sync})
    return nc


_CACHE = {}


def _host_tables():
    if "tabs" in _CACHE:
        return _CACHE["tabs"]
    out = {}
    inv_freq = (10000.0 ** (-np.arange(0, 32, 2, dtype=np.float32) / 32)).astype(np.float32)
    for hf in range(2):
        own = (NMETA + hf * NOWN + np.arange(NOWN)).astype(np.int64)
        oth_set = set((NMETA + (1 - hf) * NOWN + np.arange(NOWN)).tolist())
        npair = NOWN - 15
        paired = (L - own[:npair])
        assert all(int(p) in oth_set for p in paired)
        left = np.array(sorted(oth_set - set(paired.tolist())), np.int64)
        other = np.concatenate([paired, left])
        assert other.shape[0] == NOWN
        pos = np.concatenate([own, other, np.arange(NMETA)]).astype(np.int64)
        ang = pos.astype(np.float32)[:, None] * inv_freq[None, :]
        c = np.cos(ang).astype(np.float32).T
        s = np.sin(ang).astype(np.float32).T
        ktab = np.concatenate([c, c, -s, s], axis=0).astype(np.float32)
        qtab = np.zeros((128, NOWN), np.float32)
        qtab[64:96] = np.concatenate([c, c], axis=0)[:, :NOWN]
        qtab[96:128] = np.concatenate([-s, s], axis=0)[:, :NOWN]

        def cs(rows):
            th = ((rows[:, None] * own[None, :]) % L).astype(np.float64) * (2.0 * np.pi / L)
            return np.cos(th), np.sin(th)

        c_own, s_own = cs(own)
        c_oth, s_oth = cs(other)
        Ce = (0.5 * (c_own + c_oth)).reshape(32, 128, NQB, 512)
        So = (0.5 * (s_own - s_oth)).reshape(32, 128, NQB, 512)
        Co = (0.5 * (c_own - c_oth)).reshape(32, 128, NQB, 512)[31]
        Se = (0.5 * (s_own + s_oth)).reshape(32, 128, NQB, 512)[31]
        del c_own, s_own, c_oth, s_oth
        cm, sm = cs(np.arange(NMETA, dtype=np.int64))
        tab = np.zeros((NQB, 128, NCHK, 8, 512), ml_dtypes.bfloat16)
        for ch in range(8):
            for j in range(4):
                b = ch * 4 + j
                tab[:, :, ch, 2 * j, :] = Ce[b].transpose(1, 0, 2).astype(ml_dtypes.bfloat16)
                tab[:, :, ch, 2 * j + 1, :] = So[b].transpose(1, 0, 2).astype(ml_dtypes.bfloat16)
        tab[:, :, 8, 0, :] = Co.transpose(1, 0, 2).astype(ml_dtypes.bfloat16)
        tab[:, :, 8, 1, :] = Se.transpose(1, 0, 2).astype(ml_dtypes.bfloat16)
        tab[:, :NMETA, 8, 2, :] = cm.reshape(NMETA, NQB, 512).transpose(1, 0, 2).astype(ml_dtypes.bfloat16)
        tab[:, :NMETA, 8, 3, :] = sm.reshape(NMETA, NQB, 512).transpose(1, 0, 2).astype(ml_dtypes.bfloat16)
        del Ce, So
        out[hf] = dict(pos=pos, own=own, other=other, ktab=ktab, qtab=qtab, dft=tab)
    cidx = np.arange(128)
    thc = 2.0 * np.pi * ((cidx[:, None] * cidx[None, :]) % 128) / 128.0
    nrm = 1.0 / np.sqrt(float(L) * 128.0)
    out["cc"] = (np.cos(thc) * nrm).astype(np.float32)
    out["sc"] = (-np.sin(thc) * nrm).astype(np.float32)
    out["ident"] = np.eye(128, dtype=np.float32)
    selm = np.zeros((32, 96), np.float32)
    selm[np.arange(32), 64 + np.arange(32)] = 1.0
    out["sel"] = selm
    _CACHE["tabs"] = out
    return out


def _in_maps(x, meta_tokens, norm_pre_mix, w_in, q_a_norm, w_q_b, kv_a_norm, w_kv_b, w_fourier, b_fourier,
             mix_gain_attn, mix_gain_fourier, w_o, norm_post_mix, norm_pre_ffn, w_gate, w_up, w_down,
             norm_post_ffn):
    T = _host_tables()
    f = lambda a: np.ascontiguousarray(np.asarray(a, dtype=np.float32))
    x = f(x)
    meta = f(meta_tokens)
    shared = {
        "w_in": f(w_in[0]), "w_q_b": f(w_q_b[0]), "w_kv_b": f(w_kv_b[0]), "w_fourier": f(w_fourier[0]),
        "w_o": f(w_o[0]), "w_gate": f(w_gate[0]), "w_up": f(w_up[0]), "w_down": f(w_down[0]),
        "g_pre": f(np.asarray(norm_pre_mix[0]).reshape(8, 128).T),
        "g_q": f(np.asarray(q_a_norm[0]).reshape(2, 128).T),
        "g_kv": f(np.asarray(kv_a_norm[0]).reshape(128, 1)),
        "g_a": f(np.asarray(mix_gain_attn[0]).reshape(4, 128).T),
        "g_f": f(np.asarray(mix_gain_fourier[0]).reshape(4, 128).T),
        "b_f": f(np.asarray(b_fourier[0]).reshape(4, 128).T),
        "g_post": f(norm_post_mix[0]), "g_preffn": f(norm_pre_ffn[0]), "g_postffn": f(norm_post_ffn[0]),
        "cc": T["cc"], "sc": T["sc"], "ident": T["ident"], "sel": T["sel"],
    }
    maps = []
    for core in range(8):
        b, hf = core // 2, core % 2
        t = T[hf]
        own = x[b, t["own"] - NMETA]
        other = x[b, t["other"] - NMETA]
        xperm = np.concatenate([own, other, meta], axis=0)
        m = dict(shared)
        m.update({"xperm": np.ascontiguousarray(xperm), "ktab": t["ktab"], "qtab": t["qtab"], "dft": t["dft"]})
        maps.append(m)
    return maps


def kernel(**inputs):
    if "nc" not in _CACHE:
        _CACHE["nc"] = build_program()
    nc = _CACHE["nc"]
    maps = _in_maps(**inputs)
    res = run_bass_kernel_spmd(nc, maps, core_ids=list(range(8)))
    out = np.empty((4, SEQ, D), np.float32)
    for core in range(8):
        b, hf = core // 2, core % 2
        out[b, hf * NOWN:(hf + 1) * NOWN] = np.asarray(res.results[core]["y"], dtype=np.float32)
    return out
```

```python
import numpy as np
import ml_dtypes
import concourse.bass as bass
import concourse.mybir as mybir
from concourse.bass_utils import run_bass_kernel_spmd

F32 = mybir.dt.float32
BF16 = mybir.dt.bfloat16
AF = mybir.ActivationFunctionType
ALU = mybir.AluOpType

D = 1024
SEQ = 8192
NMETA = 16
L = SEQ + NMETA
NOWN = 4096
NQB = 8
NTB = 17
NLB = 65
H = 8
DFF = 2816
NJ = DFF // 128
EPS = 1e-6
SCALE = 1.0 / float(np.sqrt(96.0))
NCHK = 9

ENGS = ["pe", "act", "dve", "pool", "sp"]


class Prog:
    def __init__(self):
        self.ops = {e: [] for e in ENGS}
        self.last_w = {}
        self.readers = {}
        self.dma_cnt = {}
        self.pending = {e: set() for e in ENGS}

    def op(self, eng, fn, reads=(), writes=(), dma=None):
        idx = len(self.ops[eng])
        deps = set(self.pending[eng])
        self.pending[eng] = set()
        for r in reads:
            w = self.last_w.get(r)
            if w is not None:
                deps.add(w)
        for w_ in writes:
            w = self.last_w.get(w_)
            if w is not None:
                deps.add(w)
            for d in self.readers.get(w_, {}).values():
                deps.add(d)
        if dma is not None:
            cnt = self.dma_cnt.get(dma, 0) + 1
            self.dma_cnt[dma] = cnt
            tok = ("dma", dma, cnt)
            rkey = ("dma", dma)
        else:
            tok = ("eng", eng, idx)
            rkey = ("eng", eng)
        deps.discard(tok)
        self.ops[eng].append(dict(fn=fn, deps=deps, tok=tok, dma=dma))
        for r in reads:
            self.readers.setdefault(r, {})[rkey] = tok
        for w_ in writes:
            self.last_w[w_] = tok
            self.readers[w_] = {}
        return tok

    def barrier(self):
        alld = set()
        for e in ENGS:
            for i in range(len(self.ops[e]) - 1, -1, -1):
                if self.ops[e][i]["dma"] is None:
                    alld.add(self.ops[e][i]["tok"])
                    break
        for k, c in self.dma_cnt.items():
            alld.add(("dma", k, c))
        for e in ENGS:
            self.pending[e] |= alld

    def emit(self, nc, block_engines):
        pub = {e: set() for e in ENGS}
        for e in ENGS:
            for o in self.ops[e]:
                for d in o["deps"]:
                    if d[0] == "eng":
                        if d[1] == e and e == "pe":
                            continue
                        pub[d[1]].add(d[2])
        pubcount = {}
        for e in ENGS:
            c = 0
            m = {}
            for i, o in enumerate(self.ops[e]):
                if o["dma"] is None and i in pub[e]:
                    c += 1
                    m[i] = c
            pubcount[e] = m
        esem = {e: nc.alloc_semaphore("sem_" + e) for e in ENGS}
        dsem = {k: nc.alloc_semaphore("dsem_%d" % i) for i, k in enumerate(self.dma_cnt)}
        final_w = {k: self.dma_cnt.get(k, 0) for k in ("w_sp", "w_pool")}

        def run(e, eng):
            waited = {}
            for i, o in enumerate(self.ops[e]):
                need = {}
                for d in o["deps"]:
                    if d[0] == "eng":
                        if d[1] == e and e == "pe":
                            continue
                        key = ("e", d[1])
                        val = pubcount[d[1]][d[2]]
                    else:
                        key = ("d", d[1])
                        val = 16 * (final_w[d[1]] if d[1] in final_w else d[2])
                    if val > need.get(key, 0):
                        need[key] = val
                for key, val in need.items():
                    if waited.get(key, 0) >= val:
                        continue
                    waited[key] = val
                    sem = esem[key[1]] if key[0] == "e" else dsem[key[1]]
                    eng.wait_ge(sem, val)
                ins = o["fn"](eng)
                if o["dma"] is not None:
                    ins.then_inc(dsem[o["dma"]], 16)
                elif i in pub[e]:
                    ins.then_inc(esem[e], 1)

        for e in ENGS:
            block_engines[e](lambda eng, e=e: run(e, eng))


class Arena:
    def __init__(self, ar, total):
        self.ar = ar
        self.free = [(0, total)]
        self.allocs = {}

    def alloc(self, name, nbytes):
        nbytes = (nbytes + 63) // 64 * 64
        for i, (o, s) in enumerate(self.free):
            if s >= nbytes:
                self.free[i] = (o + nbytes, s - nbytes)
                self.allocs[name] = (o, nbytes)
                return o
        raise MemoryError("SBUF arena full allocating %s (%d B); free=%s" % (name, nbytes, self.free))

    def release(self, *names):
        for name in names:
            o, s = self.allocs.pop(name)
            self.free.append((o, s))
        self.free.sort()
        merged = []
        for o, s in self.free:
            if s == 0:
                continue
            if merged and merged[-1][0] + merged[-1][1] == o:
                merged[-1] = (merged[-1][0], merged[-1][1] + s)
            else:
                merged.append((o, s))
        self.free = merged

    def tile(self, name, shape, dtype):
        esz = 4 if dtype == F32 else 2
        n = 1
        for s in shape[1:]:
            n *= s
        off = self.alloc(name, n * esz)
        v = self.ar[:, off // 2: off // 2 + n * esz // 2]
        if dtype == F32:
            v = v.bitcast(F32)
        if len(shape) > 2:
            names = " ".join("a%d" % i for i in range(len(shape) - 1))
            kw = {"a%d" % i: shape[i + 1] for i in range(len(shape) - 1)}
            v = v.rearrange("p (%s) -> p %s" % (names, names), **kw)
        if shape[0] < 128:
            v = v[0:shape[0]]
        return v


def build_program(stop_after=None):
    nc = bass.Bass("TRN2", target_bir_lowering=False)
    P = Prog()
    _stop = [False]

    def din(name, shape, dt=F32):
        return nc.dram_tensor(name, list(shape), dt, kind="ExternalInput").ap()

    xperm = din("xperm", [L, D])
    w_in = din("w_in", [D, 928])
    w_q_b = din("w_q_b", [256, 768])
    w_kv_b = din("w_kv_b", [128, 1024])
    w_f = din("w_fourier", [4, 128, 128])
    w_o = din("w_o", [D, D])
    w_gate = din("w_gate", [D, DFF])
    w_up = din("w_up", [D, DFF])
    w_down = din("w_down", [DFF, D])
    d_gpre = din("g_pre", [128, 8])
    d_gq = din("g_q", [128, 2])
    d_gkv = din("g_kv", [128, 1])
    d_ga = din("g_a", [128, 4])
    d_gf = din("g_f", [128, 4])
    d_bf = din("b_f", [128, 4])
    d_gpost = din("g_post", [D])
    d_gpreffn = din("g_preffn", [D])
    d_gpostffn = din("g_postffn", [D])
    d_ktab = din("ktab", [64, L])
    d_qtab = din("qtab", [128, NOWN])
    d_cc = din("cc", [128, 128])
    d_sc = din("sc", [128, 128])
    d_ident = din("ident", [128, 128])
    d_sel = din("sel", [32, 96])
    d_dft = din("dft", [NQB, 128, NCHK, 8, 512], BF16)
    y = nc.dram_tensor("y", [NOWN, D], F32, kind="ExternalOutput").ap()
    scr_g = nc.dram_tensor("scr_g", [NJ, 128, 1024], BF16).ap()
    scr_u = nc.dram_tensor("scr_u", [NJ, 128, 1024], BF16).ap()
    scr_d = nc.dram_tensor("scr_d", [NJ, 128, 1024], BF16).ap()
    scr_f = nc.dram_tensor("scr_f", [NQB, 128, 4, 512], BF16).ap()

    TOTAL = 211968
    AR = nc.alloc_sbuf_tensor("arena", [128, TOTAL // 2], BF16)
    A = Arena(AR, TOTAL)
    PS = nc.alloc_psum_tensor("ps", [128, 4096], F32)

    def bank(b, n=1):
        return PS[:, b * 512:(b + n) * 512]

    def bank_bf(b):
        return PS[:, b * 512:(b + 1) * 512].bitcast(BF16)

    def pb(b):
        return ("ps", b)

    ident = A.tile("ident", [128, 128], BF16)
    ones = A.tile("ones", [128, 128], BF16)
    sel = A.tile("sel", [32, 96], BF16)
    epst = A.tile("eps", [128, 1], F32)
    gq = A.tile("gq", [128, 2], F32)
    gkv = A.tile("gkv", [128, 1], F32)
    ga = A.tile("ga", [128, 4], F32)
    gf = A.tile("gf", [128, 4], F32)
    bfb = A.tile("bfb", [128, 4], F32)
    gpre = A.tile("gpre", [128, 8], F32)
    Wq = A.tile("Wq", [128, 2, 8, 128], BF16)
    Wkvb = A.tile("Wkvb", [128, 1024], BF16)
    M12 = A.tile("M12", [128, 8, 128], BF16)

    def dma(q, out, in_, key, reads=(), writes=()):
        if key == "w":
            key = "w_" + q
        return P.op(q, lambda e, out=out, in_=in_: e.dma_start(out=out, in_=in_), reads=reads, writes=writes, dma=key)

    P.op("dve", lambda e: e.memset(ones, 1.0), writes=["ones"])
    P.op("dve", lambda e: e.memset(epst, EPS), writes=["eps"])
    for dst, src, nm in [(gq, d_gq, "gq"), (gkv, d_gkv, "gkv"), (ga, d_ga, "ga"), (gf, d_gf, "gf"),
                         (bfb, d_bf, "bfb"), (gpre, d_gpre, "gpre")]:
        dma("sp", dst, src, "w", writes=[nm])
    dma("pool", ident, d_ident, "w", writes=["ident"])
    dma("pool", sel, d_sel, "w", writes=["sel"])

    Win = A.tile("Win", [128, 8, 928], BF16)
    Wkr = A.tile("Wkr", [128, 8, 64], BF16)
    wst = [A.tile("wst%d" % i, [128, 928], F32) for i in range(2)]
    w_in_r = w_in.rearrange("(c p) n -> p c n", p=128)
    for c in range(8):
        s = c % 2
        dma("sp", wst[s], w_in_r[:, c, :], ("wst", s), writes=[("wst", s)])
        P.op("dve", lambda e, c=c, s=s: e.tensor_scalar(out=Win[:, c, :], in0=wst[s], scalar1=gpre[:, c:c + 1],
                                                       scalar2=None, op0=ALU.mult),
             reads=[("wst", s), "gpre"], writes=["Win"])
        P.op("dve", lambda e, c=c, s=s: e.tensor_scalar(out=Wkr[:, c, 0:32], in0=wst[s][:, 384:416],
                                                       scalar1=gpre[:, c:c + 1], scalar2=None, op0=ALU.mult),
             reads=[("wst", s), "gpre"], writes=["Wkr"])
        P.op("dve", lambda e, c=c, s=s: e.tensor_scalar(out=Wkr[:, c, 32:48], in0=wst[s][:, 400:416],
                                                       scalar1=gpre[:, c:c + 1], scalar2=None, op0=ALU.mult),
             reads=[("wst", s), "gpre"], writes=["Wkr"])
        P.op("dve", lambda e, c=c, s=s: e.tensor_scalar(out=Wkr[:, c, 48:64], in0=wst[s][:, 384:400],
                                                       scalar1=gpre[:, c:c + 1], scalar2=None, op0=ALU.mult),
             reads=[("wst", s), "gpre"], writes=["Wkr"])
    wq_r = w_q_b.rearrange("(c p) (h e) -> p c h e", p=128, e=96)
    for c in range(2):
        dma("pool", Wq[:, c, :, 0:96], wq_r[:, c, :, :], "w", writes=[("Wq", c, 0)])
        dma("pool", Wq[:, c, :, 96:112], wq_r[:, c, :, 80:96], "w", writes=[("Wq", c, 1)])
        dma("pool", Wq[:, c, :, 112:128], wq_r[:, c, :, 64:80], "w", writes=[("Wq", c, 2)])
    dma("pool", Wkvb, w_kv_b, "w", writes=["Wkvb"])
    ccs = A.tile("ccs", [128, 2, 128], BF16)
    wfs = A.tile("wfs", [128, 4, 128], BF16)
    dma("pool", ccs[:, 0, :], d_cc, "w", writes=[("ccs", 0)])
    dma("pool", ccs[:, 1, :], d_sc, "w", writes=[("ccs", 1)])
    dma("pool", wfs, w_f.rearrange("g c d -> c g d"), "w", writes=["wfs"])
    for m in range(2):
        for g in range(4):
            P.op("pe", lambda e, m=m, g=g: e.matmul(bank(m)[:, g * 128:(g + 1) * 128], ccs[:, m, :], wfs[:, g, :],
                                                    start=True, stop=True),
                 reads=[("ccs", m), "wfs"], writes=[pb(m)])
        P.op("dve", lambda e, m=m: e.tensor_copy(out=M12[:, m * 4:(m + 1) * 4, :],
                                                 in_=bank(m).rearrange("p (g d) -> p g d", d=128)),
             reads=[pb(m)], writes=["M12"])
    A.release("ccs", "wfs", "wst0", "wst1")
    P.barrier()

    if stop_after == 'W':
        return _finish(nc, P)
    ckvT = A.tile("ckvT", [128, L], BF16)
    kropeT = A.tile("kropeT", [32, L], BF16)
    cqT = A.tile("cqT", [128, 2, NOWN], BF16)
    f_sb = A.tile("f_sb", [128, NLB, 512], BF16)
    xt = [A.tile("xt%d" % i, [128, D], F32) for i in range(4)]
    hb = [A.tile("hb%d" % i, [128, D], BF16) for i in range(6)]
    hT = [A.tile("hT%d" % i, [128, 8, 512], BF16) for i in range(2)]
    junk = A.tile("junk", [128, D], BF16)
    ssq = A.tile("ssq", [128, 8], F32)
    sq = A.tile("sq", [128, 3, 512], BF16)
    lnb = A.tile("lnb", [128, 2, 512], F32)
    rbc = A.tile("rbc", [128, 2, 512], F32)
    ktb = [A.tile("ktb%d" % i, [64, 512], F32) for i in range(2)]
    krp = A.tile("krp", [64, 512], F32)
    krp2 = A.tile("krp2", [32, 512], F32)
    ftmp = [A.tile("ftmp%d" % i, [128, 512], BF16) for i in range(2)]

    def rstd_col(col, n, nt):
        P.op("act", lambda e: e.activation(out=ssq[:nt, col:col + 1], in_=ssq[:nt, col:col + 1], func=AF.Ln,
                                           scale=1.0 / n, bias=epst[:nt, :]),
             reads=[("ssq", col), "eps"], writes=[("ssq", col)])
        P.op("act", lambda e: e.activation(out=ssq[:nt, col:col + 1], in_=ssq[:nt, col:col + 1], func=AF.Exp,
                                           scale=-0.5),
             reads=[("ssq", col)], writes=[("ssq", col)])

    ptiles = []
    for tb in range(NTB):
        ntok_ = 512 if tb < 16 else 16
        for t in range((ntok_ + 127) // 128):
            ptiles.append((tb, t, min(128, ntok_ - t * 128)))
    NHB = 6
    PLEAD = 4

    def p_t1(i):
        tb, t, nt = ptiles[i]
        xs = i % 4
        bs = i % NHB
        col = i % 4
        r0 = tb * 512 + t * 128
        dma("sp", xt[xs][:nt, :], xperm[r0:r0 + nt, :], ("xt", xs), writes=[("xt", xs)])
        P.op("act", lambda e: e.activation(out=junk[:nt, :], in_=xt[xs][:nt, :], func=AF.Square,
                                           accum_out=ssq[:nt, col:col + 1]),
             reads=[("xt", xs)], writes=["junk", ("ssq", col)])
        rstd_col(col, D, nt)
        P.op("act", lambda e: e.activation(out=hb[bs][:nt, :], in_=xt[xs][:nt, :], func=AF.Identity,
                                           scale=ssq[:nt, col:col + 1]),
             reads=[("xt", xs), ("ssq", col)], writes=[("hb", bs)])

    def p_t2(i):
        tb, t, nt = ptiles[i]
        bs = i % NHB
        hs = tb % 2
        tbk = i % 2
        for c in range(8):
            P.op("pe", lambda e, c=c: e.transpose(
                bank_bf(tbk)[:, c * 128:c * 128 + nt], hb[bs][:nt, c * 128:(c + 1) * 128], ident[:nt, :nt]),
                 reads=[("hb", bs), "ident"], writes=[pb(tbk)])
        P.op("dve", lambda e: e.tensor_copy(
            out=hT[hs][:, :, t * 128:t * 128 + nt],
            in_=bank_bf(tbk).rearrange("p (c n) -> p c n", n=128)[:, :, :nt]),
             reads=[pb(tbk)], writes=[("hT", hs)])

    def p_proj(tb):
        t0 = tb * 512
        ntok = 512 if tb < 16 else 16
        ntile = (ntok + 127) // 128
        hs = tb % 2
        own = tb < NQB
        ks = tb % 2
        if tb + 1 < NTB:
            ntk = 512 if tb + 1 < 16 else 16
            dma("sp", ktb[1 - ks][:, :ntk], d_ktab[:, (tb + 1) * 512:(tb + 1) * 512 + ntk], ("ktb", 1 - ks),
                writes=[("ktb", 1 - ks)])
        for c in range(8):
            P.op("pe", lambda e, c=c, hs=hs, ntok=ntok: e.matmul(bank(2)[:, :ntok], Win[:, c, 256:384],
                                                                hT[hs][:, c, :ntok], start=(c == 0), stop=(c == 7)),
                 reads=["Win", ("hT", hs)], writes=[pb(2)])
        for c in range(8):
            P.op("pe", lambda e, c=c, hs=hs, ntok=ntok: e.matmul(bank(3)[0:64, :ntok], Wkr[:, c, :],
                                                                hT[hs][:, c, :ntok], start=(c == 0), stop=(c == 7)),
                 reads=["Wkr", ("hT", hs)], writes=[pb(3)])
        if own:
            for m in range(2):
                for c in range(8):
                    P.op("pe", lambda e, c=c, m=m, hs=hs: e.matmul(bank(6 + m), Win[:, c, m * 128:(m + 1) * 128],
                                                                  hT[hs][:, c, :], start=(c == 0), stop=(c == 7)),
                         reads=["Win", ("hT", hs)], writes=[pb(6 + m)])
        P.op("act", lambda e, ntok=ntok: e.activation(out=sq[:, 0, :ntok], in_=bank(2)[:, :ntok], func=AF.Square),
             reads=[pb(2)], writes=[("sq", 0)])
        if own:
            for m in range(2):
                P.op("act", lambda e, m=m: e.activation(out=sq[:, 1 + m, :], in_=bank(6 + m), func=AF.Square),
                     reads=[pb(6 + m)], writes=[("sq", 1 + m)])
        P.op("pe", lambda e, ntok=ntok: e.matmul(bank(4)[:, :ntok], ones, sq[:, 0, :ntok], start=True, stop=True),
             reads=["ones", ("sq", 0)], writes=[pb(4)])
        if own:
            for m in range(2):
                P.op("pe", lambda e, m=m: e.matmul(bank(5), ones, sq[:, 1 + m, :], start=(m == 0), stop=(m == 1)),
                     reads=["ones", ("sq", 1 + m)], writes=[pb(5)])
        P.op("act", lambda e, ntok=ntok: e.activation(out=lnb[:, 0, :ntok], in_=bank(4)[:, :ntok], func=AF.Ln,
                                                      scale=1.0 / 128, bias=epst),
             reads=[pb(4), "eps"], writes=[("lnb", 0)])
        P.op("act", lambda e, ntok=ntok: e.activation(out=rbc[:, 0, :ntok], in_=lnb[:, 0, :ntok], func=AF.Exp,
                                                      scale=-0.5),
             reads=[("lnb", 0)], writes=[("rbc", 0)])
        if own:
            P.op("act", lambda e: e.activation(out=lnb[:, 1, :], in_=bank(5), func=AF.Ln, scale=1.0 / 256, bias=epst),
                 reads=[pb(5), "eps"], writes=[("lnb", 1)])
            P.op("act", lambda e: e.activation(out=rbc[:, 1, :], in_=lnb[:, 1, :], func=AF.Exp, scale=-0.5),
                 reads=[("lnb", 1)], writes=[("rbc", 1)])
        P.op("dve", lambda e, t0=t0, ntok=ntok: e.scalar_tensor_tensor(
            out=ckvT[:, t0:t0 + ntok], in0=bank(2)[:, :ntok], scalar=gkv[:, 0:1], in1=rbc[:, 0, :ntok],
            op0=ALU.mult, op1=ALU.mult),
             reads=[pb(2), "gkv", ("rbc", 0)], writes=[("ckvT", tb)])
        if own:
            for m in range(2):
                P.op("dve", lambda e, m=m, t0=t0: e.scalar_tensor_tensor(
                    out=cqT[:, m, t0:t0 + 512], in0=bank(6 + m), scalar=gq[:, m:m + 1], in1=rbc[:, 1, :],
                    op0=ALU.mult, op1=ALU.mult),
                     reads=[pb(6 + m), "gq", ("rbc", 1)], writes=[("cqT", tb)])
        P.op("dve", lambda e, ks=ks, ntok=ntok: e.tensor_tensor(out=krp[0:32, :ntok], in0=bank(3)[0:32, :ntok],
                                                               in1=ktb[ks][0:32, :ntok], op=ALU.mult),
             reads=[pb(3), ("ktb", ks)], writes=["krp"])
        P.op("dve", lambda e, ks=ks, ntok=ntok: e.tensor_tensor(out=krp2[0:32, :ntok], in0=bank(3)[32:64, :ntok],
                                                               in1=ktb[ks][32:64, :ntok], op=ALU.mult),
             reads=[pb(3), ("ktb", ks)], writes=["krp2"])
        P.op("dve", lambda e, t0=t0, ntok=ntok: e.tensor_tensor(out=kropeT[:, t0:t0 + ntok], in0=krp[0:32, :ntok],
                                                               in1=krp2[0:32, :ntok], op=ALU.add),
             reads=["krp", "krp2"], writes=[("kropeT", tb)])
        for t in range(ntile):
            nt = min(128, ntok - t * 128)
            fb = t % 2
            lb = tb * 4 + t
            for c in range(8):
                P.op("pe", lambda e, c=c, hs=hs, t=t, nt=nt, fb=fb: e.matmul(
                    bank(fb)[:nt, :], hT[hs][:, c, t * 128:t * 128 + nt], Win[:, c, 416:928],
                    start=(c == 0), stop=(c == 7)),
                     reads=["Win", ("hT", hs)], writes=[pb(fb)])
            P.op("dve", lambda e, nt=nt, fb=fb, lb=lb: e.tensor_copy(out=f_sb[:nt, lb, :], in_=bank(fb)[:nt, :]),
                 reads=[pb(fb)], writes=[("f", lb)])
            if 32 <= lb < 64:
                b = lb - 32
                fs_ = b % 2
                P.op("dve", lambda e, b=b, fs_=fs_: e.tensor_tensor(out=ftmp[fs_], in0=f_sb[:, b, :],
                                                                  in1=f_sb[:, b + 32, :], op=ALU.subtract),
                     reads=[("f", b), ("f", b + 32)], writes=[("ftmp", fs_)])
                P.op("dve", lambda e, b=b: e.tensor_tensor(out=f_sb[:, b, :], in0=f_sb[:, b, :],
                                                           in1=f_sb[:, b + 32, :], op=ALU.add),
                     reads=[("f", b), ("f", b + 32)], writes=[("f", b)])
                P.op("pool", lambda e, b=b, fs_=fs_: e.tensor_copy(out=f_sb[:, b + 32, :], in_=ftmp[fs_]),
                     reads=[("ftmp", fs_)], writes=[("f", b + 32)])

    dma("sp", ktb[0], d_ktab[:, 0:512], ("ktb", 0), writes=[("ktb", 0)])
    for i0 in range(PLEAD):
        p_t1(i0)
    for i in range(len(ptiles)):
        if i + PLEAD < len(ptiles):
            p_t1(i + PLEAD)
        p_t2(i)
        if i + 1 == len(ptiles) or ptiles[i + 1][0] != ptiles[i][0]:
            p_proj(ptiles[i][0])
    A.release("Win", "Wkr", "xt0", "xt1", "xt2", "xt3", "hb0", "hb1", "hb2", "hb3", "hb4", "hb5", "hT0", "hT1", "junk", "sq", "lnb", "rbc",
              "ktb0", "ktb1", "krp", "krp2", "ftmp0", "ftmp1")
    P.barrier()

    if stop_after == 'P':
        return _finish(nc, P)
    fNb = [A.tile("fNb%d" % i, [128, 4, 512], BF16) for i in range(2)]
    tabs = [A.tile("tab%d" % i, [128, 8, 512], BF16) for i in range(3)]
    ABs = A.tile("ABs", [128, 8, 512], BF16)
    ysb = A.tile("ysb", [128, 4, 512], F32)
    ysq = A.tile("ysq", [128, 4, 512], BF16)
    lnf = A.tile("lnf", [128, 512], F32)
    rbf = A.tile("rbf", [128, 512], F32)
    chunk_state = [0]

    def f_load(kb, ch):
        ts_ = chunk_state[0] % 3
        chunk_state[0] += 1
        dma("sp", tabs[ts_], d_dft[kb, :, ch, :, :], ("tab", ts_), writes=[("tab", ts_)])
        return ts_

    def f_mm_a(ch, ts_):
        for j in range(4):
            b = ch * 4 + j
            for g in range(4):
                P.op("pe", lambda e, j=j, b=b, g=g: e.matmul(
                    bank(g), f_sb[:, b, g * 128:(g + 1) * 128], tabs[ts_][:, 2 * j, :],
                    start=(b == 0), stop=False),
                     reads=[("f", b), ("tab", ts_)], writes=[pb(g)])

    def f_mm_b(ch, ts_, groups):
        for j in range(4):
            b = ch * 4 + j
            for g in groups:
                P.op("pe", lambda e, j=j, b=b, g=g: e.matmul(
                    bank(4 + g), f_sb[:, 32 + b, g * 128:(g + 1) * 128], tabs[ts_][:, 2 * j + 1, :],
                    start=(b == 0), stop=False),
                     reads=[("f", 32 + b), ("tab", ts_)], writes=[pb(4 + g)])

    def f_mm_edge(ts_):
        for g in range(4):
            P.op("pe", lambda e, g=g: e.matmul(
                bank(g), f_sb[:, 63, g * 128:(g + 1) * 128], tabs[ts_][:, 0, :], start=False, stop=False),
                 reads=[("f", 63), ("tab", ts_)], writes=[pb(g)])
            P.op("pe", lambda e, g=g: e.matmul(
                bank(4 + g), f_sb[:, 31, g * 128:(g + 1) * 128], tabs[ts_][:, 1, :], start=False, stop=False),
                 reads=[("f", 31), ("tab", ts_)], writes=[pb(4 + g)])
            P.op("pe", lambda e, g=g: e.matmul(
                bank(g), f_sb[:16, 64, g * 128:(g + 1) * 128], tabs[ts_][:16, 2, :], start=False, stop=True),
                 reads=[("f", 64), ("tab", ts_)], writes=[pb(g)])
            P.op("pe", lambda e, g=g: e.matmul(
                bank(4 + g), f_sb[:16, 64, g * 128:(g + 1) * 128], tabs[ts_][:16, 3, :], start=False, stop=True),
                 reads=[("f", 64), ("tab", ts_)], writes=[pb(4 + g)])

    def f_tail_evac():
        for g in range(4):
            P.op("act", lambda e, g=g: e.copy(out=ABs[:, g, :], in_=bank(g)), reads=[pb(g)], writes=[("ABs", g)])
            P.op("dve", lambda e, g=g: e.tensor_copy(out=ABs[:, 4 + g, :], in_=bank(4 + g)), reads=[pb(4 + g)],
                 writes=[("ABs", 4 + g)])

    def f_tail_y():
        for g in range(4):
            P.op("pe", lambda e, g=g: e.matmul(bank(4 + g), M12[:, g, :], ABs[:, g, :], start=True, stop=False),
                 reads=["M12", ("ABs", g)], writes=[pb(4 + g)])
            P.op("pe", lambda e, g=g: e.matmul(bank(4 + g), M12[:, 4 + g, :], ABs[:, 4 + g, :], start=False,
                                               stop=True),
                 reads=["M12", ("ABs", 4 + g)], writes=[pb(4 + g)])
            P.op("act", lambda e, g=g: e.activation(out=ysb[:, g, :], in_=bank(4 + g), func=AF.Identity,
                                                    bias=bfb[:, g:g + 1]),
                 reads=[pb(4 + g), "bfb"], writes=[("ysb", g)])
            P.op("dve", lambda e, g=g: e.tensor_tensor(out=ysq[:, g, :], in0=ysb[:, g, :], in1=ysb[:, g, :],
                                                       op=ALU.mult),
                 reads=[("ysb", g)], writes=[("ysq", g)])

    def f_tail_stats(kb):
        for g in range(4):
            P.op("pe", lambda e, g=g: e.matmul(bank(7), ones, ysq[:, g, :], start=(g == 0), stop=(g == 3)),
                 reads=["ones", ("ysq", g)], writes=[pb(7)])
        P.op("act", lambda e: e.activation(out=lnf, in_=bank(7), func=AF.Ln, scale=1.0 / 512, bias=epst),
             reads=[pb(7), "eps"], writes=["lnf"])
        P.op("act", lambda e: e.activation(out=rbf, in_=lnf, func=AF.Exp, scale=-0.5), reads=["lnf"], writes=["rbf"])
        fs = kb % 2
        for g in range(4):
            P.op("dve", lambda e, g=g: e.scalar_tensor_tensor(
                out=fNb[fs][:, g, :], in0=ysb[:, g, :], scalar=gf[:, g:g + 1], in1=rbf,
                op0=ALU.mult, op1=ALU.mult),
                 reads=[("ysb", g), "gf", "rbf"], writes=[("fNb", fs)])
        dma("pool", scr_f[kb], fNb[fs], ("fNout", fs), reads=[("fNb", fs)], writes=["scr_f"])

    for kb in range(NQB):
        ts0 = f_load(kb, 0)
        if kb == 0:
            f_mm_a(0, ts0)
            f_mm_b(0, ts0, [0, 1, 2, 3])
        else:
            f_mm_a(0, ts0)
            f_tail_y()
            f_mm_b(0, ts0, [0, 1, 2])
            f_tail_stats(kb - 1)
            f_mm_b(0, ts0, [3])
        for ch in range(1, 8):
            ts_ = f_load(kb, ch)
            f_mm_a(ch, ts_)
            f_mm_b(ch, ts_, [0, 1, 2, 3])
        ts_ = f_load(kb, 8)
        f_mm_edge(ts_)
        f_tail_evac()
    f_tail_y()
    f_tail_stats(NQB - 1)
    A.release("f_sb", "tab0", "tab1", "tab2", "ABs", "ysb", "ysq", "lnf", "rbf", "fNb0", "fNb1")
    P.barrier()

    if stop_after == 'F':
        return _finish(nc, P)
    attnT = A.tile("attnT", [128, 4, NOWN], BF16)
    KT = [A.tile("KT%d" % i, [96, L], BF16) for i in range(2)]
    VA = [A.tile("VA%d" % i, [128, NLB, 128], BF16) for i in range(2)]
    QT = [A.tile("QT%d" % i, [96, 512], BF16) for i in range(2)]
    PT = [A.tile("PT%d" % i, [128, 1024], BF16) for i in range(3)]
    qtab = A.tile("qtab", [128, NOWN], F32)
    tq = A.tile("tq", [128, 2, 512], F32)
    rcp = A.tile("rcp", [64, 512], F32)
    stg = [A.tile("stg%d" % i, [128, 1024], BF16) for i in range(4)]
    dma("sp", qtab, d_qtab, "qtab", writes=["qtab"])
    for i in range(2):
        P.op("pool", lambda e, i=i: e.memset(VA[i][:, :, 64:128], 1.0), writes=[("VAones", i)])

    wg_r = w_gate.rearrange("(c p) (j n) -> p j c n", p=128, n=128)
    wu_r = w_up.rearrange("(c p) (j n) -> p j c n", p=128, n=128)
    wd_r = w_down.rearrange("(j p) n -> p j n", p=128)
    sctr = 0
    for j in range(NJ):
        for src, dst, is3 in [(wg_r, scr_g, True), (wu_r, scr_u, True), (wd_r, scr_d, False)]:
            s = sctr % 4
            sctr += 1
            if is3:
                dma("pool", stg[s].rearrange("p (c n) -> p c n", n=128), src[:, j, :, :], ("stgin", s),
                    writes=[("stg", s)])
            else:
                dma("pool", stg[s], src[:, j, :], ("stgin", s), writes=[("stg", s)])
            dma("sp", dst[j], stg[s], ("stgout", s), reads=[("stg", s)], writes=["scr"])

    osb = A.tile("osb", [128, 512], F32)
    JB = 7
    OB = 6

    def kv_steps(h, banks=(7,), use_act=False):
        kv = h % 2
        steps = []
        bctr = [0]
        for tb in range(NTB):
            def st(tb=tb):
                t0 = tb * 512
                ntok = 512 if tb < 16 else 16
                JB = banks[bctr[0] % len(banks)]
                bctr[0] += 1
                P.op("pe", lambda e: e.matmul(bank(JB)[0:64, :ntok], Wkvb[:, h * 128:h * 128 + 64],
                                              ckvT[:, t0:t0 + ntok], start=True, stop=True),
                     reads=["Wkvb", ("ckvT", tb)], writes=[pb(JB)])
                if use_act and tb % 2 == 1:
                    P.op("act", lambda e: e.copy(out=KT[kv][0:64, t0:t0 + ntok], in_=bank(JB)[0:64, :ntok]),
                         reads=[pb(JB)], writes=[("KT", kv)])
                else:
                    P.op("dve", lambda e: e.tensor_copy(out=KT[kv][0:64, t0:t0 + ntok], in_=bank(JB)[0:64, :ntok]),
                         reads=[pb(JB)], writes=[("KT", kv)])
                if h < 2:
                    P.op("dve", lambda e: e.tensor_copy(out=KT[kv][64:96, t0:t0 + ntok],
                                                        in_=kropeT[:, t0:t0 + ntok]),
                         reads=[("kropeT", tb)], writes=[("KT", kv)])
            steps.append(st)
        for l0 in range(0, NLB, 8):
            def st(l0=l0):
                nl = min(8, NLB - l0)
                JB = banks[bctr[0] % len(banks)]
                bctr[0] += 1
                for li in range(nl):
                    lb = l0 + li
                    kp = 128 if lb < 64 else 16
                    P.op("pe", lambda e, lb=lb, li=li, kp=kp: e.matmul(
                        bank(JB)[:kp, li * 64:(li + 1) * 64], ckvT[:, lb * 128:lb * 128 + kp],
                        Wkvb[:, h * 128 + 64:h * 128 + 128], start=True, stop=True),
                         reads=["Wkvb"] + [("ckvT", lb // 4)], writes=[pb(JB)])
                if l0 + nl <= 64 and use_act and (l0 // 8) % 2 == 0:
                    P.op("act", lambda e: e.copy(
                        out=VA[kv][:, l0:l0 + nl, 0:64],
                        in_=bank(JB).rearrange("p (l d) -> p l d", d=64)[:, 0:nl, :]),
                         reads=[pb(JB)], writes=[("VA", kv)])
                elif l0 + nl <= 64:
                    P.op("dve", lambda e: e.tensor_copy(
                        out=VA[kv][:, l0:l0 + nl, 0:64],
                        in_=bank(JB).rearrange("p (l d) -> p l d", d=64)[:, 0:nl, :]),
                         reads=[pb(JB)], writes=[("VA", kv)])
                else:
                    P.op("dve", lambda e: e.tensor_copy(out=VA[kv][0:16, l0, 0:64], in_=bank(JB)[0:16, 0:64]),
                         reads=[pb(JB)], writes=[("VA", kv)])
            steps.append(st)
        return steps

    def q_build(h, qb, qs):
        q0 = qb * 512
        for c in range(2):
            P.op("pe", lambda e, c=c: e.matmul(bank(JB), Wq[:, c, h, :], cqT[:, c, q0:q0 + 512],
                                               start=(c == 0), stop=(c == 1)),
                 reads=[("Wq", c, 0), ("Wq", c, 1), ("Wq", c, 2), ("cqT", qb)], writes=[pb(JB)])
        P.op("dve", lambda e: e.tensor_copy(out=QT[qs][0:64, :], in_=bank(JB)[0:64, :]),
             reads=[pb(JB)], writes=[("QT", qs)])
        P.op("dve", lambda e: e.tensor_tensor(out=tq[64:96, 0, :], in0=bank(JB)[64:96, :],
                                              in1=qtab[64:96, q0:q0 + 512], op=ALU.mult),
             reads=[pb(JB), "qtab"], writes=["tq0"])
        P.op("dve", lambda e: e.tensor_tensor(out=tq[64:96, 1, :], in0=bank(JB)[96:128, :],
                                              in1=qtab[96:128, q0:q0 + 512], op=ALU.mult),
             reads=[pb(JB), "qtab"], writes=["tq1"])
        P.op("dve", lambda e: e.tensor_tensor(out=QT[qs][64:96, :], in0=tq[64:96, 0, :], in1=tq[64:96, 1, :],
                                              op=ALU.add),
             reads=["tq0", "tq1"], writes=[("QT", qs)])

    items = [(h, qb) for h in range(H) for qb in range(NQB)]
    for st in kv_steps(0, banks=(7, 6, 5, 4), use_act=True):
        st()
    q_build(0, 0, 0)
    pending = []
    def attn_item(it, h, qb, pending):
        kv = h % 2
        q0 = qb * 512
        qs = it % 2

        def mm_s(kg):
            g = kg % 3
            for i in range(2):
                lb = kg * 2 + i
                P.op("pe", lambda e, lb=lb, i=i, g=g: e.matmul(
                    bank(g * 2 + i), KT[kv][:, lb * 128:(lb + 1) * 128], QT[qs], start=True, stop=True),
                     reads=[("KT", kv), ("QT", qs)], writes=[pb(g * 2 + i)])

        mm_s(0)
        mm_s(1)
        for kg in range(32):
            g = kg % 3
            ps_ = kg % 3
            if kg + 2 < 32:
                mm_s(kg + 2)
            elif kg + 2 == 32:
                gm = 32 % 3
                P.op("pe", lambda e, gm=gm: e.matmul(bank(gm * 2)[0:16, :], KT[kv][:, 8192:8208], QT[qs],
                                                     start=True, stop=True),
                     reads=[("KT", kv), ("QT", qs)], writes=[pb(gm * 2)])
            P.op("act", lambda e, g=g, ps_=ps_: e.activation(out=PT[ps_], in_=bank(g * 2, 2), func=AF.Exp,
                                                            scale=SCALE),
                 reads=[pb(g * 2), pb(g * 2 + 1)], writes=[("PT", ps_)])
            for i in range(2):
                lb = kg * 2 + i
                P.op("pe", lambda e, lb=lb, i=i, ps_=ps_: e.matmul(
                    bank(OB), VA[kv][:, lb, :], PT[ps_][:, i * 512:(i + 1) * 512], start=(lb == 0), stop=False),
                     reads=[("VA", kv), ("VAones", kv), ("PT", ps_)], writes=[pb(OB)])
            if kg == 6 and it + 1 < len(items):
                q_build(items[it + 1][0], items[it + 1][1], (it + 1) % 2)
            if kg >= 8 and kg % 4 == 0 and pending and qb >= 1:
                pending.pop(0)()
        gm = 32 % 3
        pm = 32 % 3
        P.op("act", lambda e, gm=gm, pm=pm: e.activation(out=PT[pm][0:16, 0:512], in_=bank(gm * 2)[0:16, :],
                                                        func=AF.Exp, scale=SCALE),
             reads=[pb(gm * 2)], writes=[("PT", pm)])
        P.op("pe", lambda e, pm=pm: e.matmul(bank(OB), VA[kv][0:16, 64, :], PT[pm][0:16, 0:512],
                                             start=False, stop=True),
             reads=[("VA", kv), ("VAones", kv), ("PT", pm)], writes=[pb(OB)])
        P.op("dve", lambda e: e.tensor_copy(out=osb, in_=bank(OB)), reads=[pb(OB)], writes=["osb"])
        P.op("dve", lambda e: e.reciprocal(out=rcp, in_=osb[64:128, :]), reads=["osb"], writes=["rcp"])
        po = (h % 2) * 64
        P.op("dve", lambda e, h=h, po=po, q0=q0: e.tensor_tensor(
            out=attnT[po:po + 64, h // 2, q0:q0 + 512], in0=osb[0:64, :], in1=rcp, op=ALU.mult),
             reads=["osb", "rcp"], writes=["attnT"])
        if qb == NQB - 1:
            while pending:
                pending.pop(0)()
    for it, (h, qb) in enumerate(items):
        if qb == 0 and h + 1 < H:
            pending = kv_steps(h + 1)
        attn_item(it, h, qb, pending)
    A.release("KT0", "KT1", "VA0", "VA1", "QT0", "QT1", "PT0", "PT1", "PT2", "qtab", "tq", "rcp", "osb",
              "stg0", "stg1", "stg2", "stg3", "ckvT", "kropeT", "cqT")
    P.barrier()

    if stop_after == 'A':
        return _finish(nc, P)
    hfT = A.tile("hfT", [128, 8, NOWN], BF16)
    Wo = A.tile("Wo", [128, 8, D], BF16)
    gpost = A.tile("gpost", [128, D], F32)
    gpreffn = A.tile("gpreffn", [128, D], F32)
    mT = [A.tile("mT%d" % i, [128, 4, 512], BF16) for i in range(2)]
    fN = [A.tile("fN%d" % i, [128, 4, 512], BF16) for i in range(2)]
    asq = A.tile("asq", [128, 4, 512], BF16)
    lna = A.tile("lna", [128, 512], F32)
    rba = A.tile("rba", [128, 512], F32)
    NOB = 4
    xo = [A.tile("xo%d" % i, [128, D], F32) for i in range(NOB)]
    r1 = [A.tile("r1%d" % i, [128, D], F32) for i in range(NOB)]
    hfb = [A.tile("hfb%d" % i, [128, D], BF16) for i in range(NOB)]
    junk2 = A.tile("junk2", [128, D], BF16)
    junk2b = A.tile("junk2b", [128, D], BF16)
    ssq2 = A.tile("ssq2", [128, 8], F32)
    dma("pool", Wo, w_o.rearrange("(c p) n -> p c n", p=128), "wo", writes=["Wo"])
    dma("sp", gpost, d_gpost.partition_broadcast(128), "gp_a", writes=["gpost"])
    dma("sp", gpreffn, d_gpreffn.partition_broadcast(128), "gp_b", writes=["gpreffn"])

    def rstd2(col, n):
        P.op("act", lambda e: e.activation(out=ssq2[:, col:col + 1], in_=ssq2[:, col:col + 1], func=AF.Ln,
                                           scale=1.0 / n, bias=epst),
             reads=[("ssq2", col), "eps"], writes=[("ssq2", col)])
        P.op("act", lambda e: e.activation(out=ssq2[:, col:col + 1], in_=ssq2[:, col:col + 1], func=AF.Exp,
                                           scale=-0.5),
             reads=[("ssq2", col)], writes=[("ssq2", col)])

    def o_prefix(qb):
        q0 = qb * 512
        fs = qb % 2
        dma("sp", fN[fs], scr_f[qb], ("fNin", fs), reads=["scr_f"], writes=[("fN", fs)])
        for pr in range(4):
            P.op("act", lambda e, pr=pr: e.activation(out=asq[:, pr, :], in_=attnT[:, pr, q0:q0 + 512],
                                                      func=AF.Square),
                 reads=["attnT"], writes=[("asq", pr)])
            P.op("pe", lambda e, pr=pr: e.matmul(bank(6), ones, asq[:, pr, :], start=(pr == 0), stop=(pr == 3)),
                 reads=["ones", ("asq", pr)], writes=[pb(6)])
        P.op("act", lambda e: e.activation(out=lna, in_=bank(6), func=AF.Ln, scale=1.0 / 512, bias=epst),
             reads=[pb(6), "eps"], writes=["lna"])
        P.op("act", lambda e: e.activation(out=rba, in_=lna, func=AF.Exp, scale=-0.5), reads=["lna"], writes=["rba"])
        for pr in range(4):
            P.op("dve", lambda e, pr=pr: e.scalar_tensor_tensor(
                out=mT[fs][:, pr, :], in0=attnT[:, pr, q0:q0 + 512], scalar=ga[:, pr:pr + 1], in1=rba,
                op0=ALU.mult, op1=ALU.mult),
                 reads=["attnT", "ga", "rba"], writes=[("mT", fs, pr)])

    def o_tile_a(qb, t, tctr):
        q0 = qb * 512
        fs = qb % 2
        s = tctr % NOB
        pbk = (tctr % 2) * 2
        c0 = (tctr % NOB) * 2
        r0 = q0 + t * 128
        if tctr + 3 < len(otiles):
            qn, tn = otiles[tctr + 3]
            rn = qn * 512 + tn * 128
            sn = (tctr + 3) % NOB
            dma("sp", xo[sn], xperm[rn:rn + 128, :], ("xo", sn), writes=[("xo", sn)])
        for hf in range(2):
            bk = pbk + hf
            for c in range(8):
                lhs = (lambda c=c: mT[fs][:, c, t * 128:(t + 1) * 128]) if c < 4 else \
                    (lambda c=c: fN[fs][:, c - 4, t * 128:(t + 1) * 128])
                P.op("pe", lambda e, lhs=lhs, c=c, hf=hf, bk=bk: e.matmul(
                    bank(bk), lhs(), Wo[:, c, hf * 512:(hf + 1) * 512], start=(c == 0), stop=(c == 7)),
                     reads=["Wo", ("fN", fs)] + [("mT", fs, c) for c in range(4)], writes=[pb(bk)])
        P.op("act", lambda e: e.activation(out=junk2, in_=bank(pbk, 2), func=AF.Square,
                                           accum_out=ssq2[:, c0:c0 + 1]),
             reads=[pb(pbk), pb(pbk + 1)], writes=["junk2", ("ssq2", c0)])
        rstd2(c0, D)
        P.op("dve", lambda e: e.scalar_tensor_tensor(
            out=r1[s], in0=bank(pbk, 2), scalar=ssq2[:, c0:c0 + 1], in1=gpost, op0=ALU.mult, op1=ALU.mult),
             reads=[pb(pbk), pb(pbk + 1), ("ssq2", c0), "gpost"], writes=[("r1", s)])
        P.op("dve", lambda e: e.tensor_tensor(out=r1[s], in0=r1[s], in1=xo[s], op=ALU.add),
             reads=[("r1", s), ("xo", s)], writes=[("r1", s)])
        dma("pool", y[r0:r0 + 128, :], r1[s], ("h1out", s), reads=[("r1", s)], writes=[("yrow", r0)])

    def o_tile_b(qb, t, tctr):
        q0 = qb * 512
        s = tctr % NOB
        c0 = (tctr % NOB) * 2
        r0 = q0 + t * 128
        P.op("act", lambda e: e.activation(out=junk2b, in_=r1[s], func=AF.Square,
                                           accum_out=ssq2[:, c0 + 1:c0 + 2]),
             reads=[("r1", s)], writes=["junk2b", ("ssq2", c0 + 1)])
        rstd2(c0 + 1, D)
        P.op("dve", lambda e: e.scalar_tensor_tensor(
            out=hfb[s], in0=r1[s], scalar=ssq2[:, c0 + 1:c0 + 2], in1=gpreffn, op0=ALU.mult, op1=ALU.mult),
             reads=[("r1", s), ("ssq2", c0 + 1), "gpreffn"], writes=[("hfb", s)])

    def o_tile_b2(qb, t, tctr):
        q0 = qb * 512
        s = tctr % NOB
        r0 = q0 + t * 128
        tbk = 4 + (tctr % 2)
        for c in range(8):
            P.op("pe", lambda e, c=c: e.transpose(bank_bf(tbk)[:, c * 128:(c + 1) * 128],
                                                  hfb[s][:, c * 128:(c + 1) * 128], ident),
                 reads=[("hfb", s), "ident"], writes=[pb(tbk)])
        P.op("dve", lambda e: e.tensor_copy(
            out=hfT[:, :, r0:r0 + 128], in_=bank_bf(tbk).rearrange("p (c n) -> p c n", n=128)),
             reads=[pb(tbk)], writes=[("hfT", qb)])

    otiles = [(qb, t) for qb in range(NQB) for t in range(4)]
    for i0 in range(3):
        dma("sp", xo[i0], xperm[i0 * 128:(i0 + 1) * 128, :], ("xo", i0), writes=[("xo", i0)])
    o_prefix(0)
    o_tile_a(otiles[0][0], otiles[0][1], 0)
    for i in range(len(otiles) + 1):
        if i < len(otiles):
            qb, t = otiles[i]
            if t == 0 and qb + 1 < NQB:
                o_prefix(qb + 1)
        if i + 1 < len(otiles):
            o_tile_a(otiles[i + 1][0], otiles[i + 1][1], i + 1)
        if i < len(otiles):
            o_tile_b(otiles[i][0], otiles[i][1], i)
        if i >= 1:
            o_tile_b2(otiles[i - 1][0], otiles[i - 1][1], i - 1)
    A.release("Wo", "gpost", "gpreffn", "mT0", "mT1", "asq", "lna", "rba", "xo0", "xo1", "xo2", "xo3", "r10", "r11",
              "r12", "r13", "hfb0", "hfb1", "hfb2", "hfb3",
              "junk2", "junk2b", "attnT", "fN0", "fN1")
    P.barrier()

    if stop_after == 'O':
        return _finish(nc, P)
    Wd = A.tile("Wd", [128, NJ, D], BF16)
    act_sb = A.tile("act_sb", [128, NJ, 512], BF16)
    wgu = [A.tile("wgu%d" % i, [128, 2, 8, 128], BF16) for i in range(3)]
    sg = [A.tile("sg%d" % i, [128, 512], F32) for i in range(2)]
    gpostffn = A.tile("gpostffn", [128, D], F32)
    h1 = [A.tile("h1%d" % i, [128, D], F32) for i in range(2)]
    fo = [A.tile("fo%d" % i, [128, D], F32) for i in range(2)]
    junk3 = A.tile("junk3", [128, D], BF16)
    dma("sp", gpostffn, d_gpostffn.partition_broadcast(128), "gp2", writes=["gpostffn"])
    wctr = 0
    tctr = 0
    for qb in range(NQB):
        q0 = qb * 512
        for j in range(NJ):
            ws = wctr % 3
            gb = (wctr % 2) * 2
            ss = wctr % 2
            wctr += 1
            dma("sp", wgu[ws][:, 0, :, :].rearrange("p c n -> p (c n)"), scr_g[j], ("wgu", ws), reads=["scr"],
                writes=[("wgu", ws)])
            dma("sp", wgu[ws][:, 1, :, :].rearrange("p c n -> p (c n)"), scr_u[j], ("wgu", ws), reads=["scr"],
                writes=[("wgu", ws)])
            if qb == 0:
                dma("pool", Wd[:, j, :], scr_d[j], ("wd", j), reads=["scr"], writes=[("Wd", j)])
            for gu in range(2):
                for c in range(8):
                    P.op("pe", lambda e, gu=gu, c=c, ws=ws, gb=gb, q0=q0: e.matmul(
                        bank(gb + gu), wgu[ws][:, gu, c, :], hfT[:, c, q0:q0 + 512], start=(c == 0), stop=(c == 7)),
                         reads=[("wgu", ws), ("hfT", qb)], writes=[pb(gb + gu)])
            P.op("act", lambda e, gb=gb, ss=ss: e.activation(out=sg[ss], in_=bank(gb), func=AF.Silu),
                 reads=[pb(gb)], writes=[("sg", ss)])
            P.op("dve", lambda e, gb=gb, ss=ss, j=j: e.tensor_tensor(out=act_sb[:, j, :], in0=sg[ss],
                                                                    in1=bank(gb + 1), op=ALU.mult),
                 reads=[("sg", ss), pb(gb + 1)], writes=[("act", j)])
        for t in range(4):
            s = tctr % 2
            tctr += 1
            r0 = q0 + t * 128
            dma("sp", h1[s], y[r0:r0 + 128, :], ("h1in", s), reads=[("yrow", r0)], writes=[("h1", s)])
            for hf in range(2):
                bk = 4 + 2 * s + hf
                for j in range(NJ):
                    P.op("pe", lambda e, j=j, t=t, hf=hf, bk=bk: e.matmul(
                        bank(bk), act_sb[:, j, t * 128:(t + 1) * 128], Wd[:, j, hf * 512:(hf + 1) * 512],
                        start=(j == 0), stop=(j == NJ - 1)),
                         reads=[("Wd", j), ("act", j)], writes=[pb(bk)])
            bk0 = 4 + 2 * s
            col = 4 + s
            P.op("act", lambda e, bk0=bk0, col=col: e.activation(out=junk3, in_=bank(bk0, 2), func=AF.Square,
                                                                accum_out=ssq2[:, col:col + 1]),
                 reads=[pb(bk0), pb(bk0 + 1)], writes=["junk3", ("ssq2", col)])
            rstd2(col, D)
            P.op("dve", lambda e, s=s, bk0=bk0, col=col: e.scalar_tensor_tensor(
                out=fo[s], in0=bank(bk0, 2), scalar=ssq2[:, col:col + 1], in1=gpostffn, op0=ALU.mult, op1=ALU.mult),
                 reads=[pb(bk0), pb(bk0 + 1), ("ssq2", col), "gpostffn"], writes=[("fo", s)])
            P.op("dve", lambda e, s=s: e.tensor_tensor(out=fo[s], in0=fo[s], in1=h1[s], op=ALU.add),
                 reads=[("fo", s), ("h1", s)], writes=[("fo", s)])
            dma("pool", y[r0:r0 + 128, :], fo[s], ("yout", s), reads=[("fo", s), ("h1", s)], writes=[("yrow", r0)])
    return _finish(nc, P)


def _finish(nc, P):
    P.barrier()
    for e in ENGS:
        P.op(e, lambda eng: eng.nop(), reads=(), writes=())
    with nc.Block() as block:
        P.emit(nc, {"pe": block.tensor, "act": block.scalar, "dve": block.vector, "pool": block.gpsimd,
                    "sp": block.sync})
    return nc


_CACHE = {}


def _host_tables():
    if "tabs" in _CACHE:
        return _CACHE["tabs"]
    out = {}
    inv_freq = (10000.0 ** (-np.arange(0, 32, 2, dtype=np.float32) / 32)).astype(np.float32)
    for hf in range(2):
        own = (NMETA + hf * NOWN + np.arange(NOWN)).astype(np.int64)
        oth_set = set((NMETA + (1 - hf) * NOWN + np.arange(NOWN)).tolist())
        npair = NOWN - 15
        paired = (L - own[:npair])
        assert all(int(p) in oth_set for p in paired)
        left = np.array(sorted(oth_set - set(paired.tolist())), np.int64)
        other = np.concatenate([paired, left])
        assert other.shape[0] == NOWN
        pos = np.concatenate([own, other, np.arange(NMETA)]).astype(np.int64)
        ang = pos.astype(np.float32)[:, None] * inv_freq[None, :]
        c = np.cos(ang).astype(np.float32).T
        s = np.sin(ang).astype(np.float32).T
        ktab = np.concatenate([c, c, -s, s], axis=0).astype(np.float32)
        qtab = np.zeros((128, NOWN), np.float32)
        qtab[64:96] = np.concatenate([c, c], axis=0)[:, :NOWN]
        qtab[96:128] = np.concatenate([-s, s], axis=0)[:, :NOWN]

        def cs(rows):
            th = ((rows[:, None] * own[None, :]) % L).astype(np.float64) * (2.0 * np.pi / L)
            return np.cos(th), np.sin(th)

        c_own, s_own = cs(own)
        c_oth, s_oth = cs(other)
        Ce = (0.5 * (c_own + c_oth)).reshape(32, 128, NQB, 512)
        So = (0.5 * (s_own - s_oth)).reshape(32, 128, NQB, 512)
        Co = (0.5 * (c_own - c_oth)).reshape(32, 128, NQB, 512)[31]
        Se = (0.5 * (s_own + s_oth)).reshape(32, 128, NQB, 512)[31]
        del c_own, s_own, c_oth, s_oth
        cm, sm = cs(np.arange(NMETA, dtype=np.int64))
        tab = np.zeros((NQB, 128, NCHK, 8, 512), ml_dtypes.bfloat16)
        for ch in range(8):
            for j in range(4):
                b = ch * 4 + j
                tab[:, :, ch, 2 * j, :] = Ce[b].transpose(1, 0, 2).astype(ml_dtypes.bfloat16)
                tab[:, :, ch, 2 * j + 1, :] = So[b].transpose(1, 0, 2).astype(ml_dtypes.bfloat16)
        tab[:, :, 8, 0, :] = Co.transpose(1, 0, 2).astype(ml_dtypes.bfloat16)
        tab[:, :, 8, 1, :] = Se.transpose(1, 0, 2).astype(ml_dtypes.bfloat16)
        tab[:, :NMETA, 8, 2, :] = cm.reshape(NMETA, NQB, 512).transpose(1, 0, 2).astype(ml_dtypes.bfloat16)
        tab[:, :NMETA, 8, 3, :] = sm.reshape(NMETA, NQB, 512).transpose(1, 0, 2).astype(ml_dtypes.bfloat16)
        del Ce, So
        out[hf] = dict(pos=pos, own=own, other=other, ktab=ktab, qtab=qtab, dft=tab)
    cidx = np.arange(128)
    thc = 2.0 * np.pi * ((cidx[:, None] * cidx[None, :]) % 128) / 128.0
    nrm = 1.0 / np.sqrt(float(L) * 128.0)
    out["cc"] = (np.cos(thc) * nrm).astype(np.float32)
    out["sc"] = (-np.sin(thc) * nrm).astype(np.float32)
    out["ident"] = np.eye(128, dtype=np.float32)
    selm = np.zeros((32, 96), np.float32)
    selm[np.arange(32), 64 + np.arange(32)] = 1.0
    out["sel"] = selm
    _CACHE["tabs"] = out
    return out


def _in_maps(x, meta_tokens, norm_pre_mix, w_in, q_a_norm, w_q_b, kv_a_norm, w_kv_b, w_fourier, b_fourier,
             mix_gain_attn, mix_gain_fourier, w_o, norm_post_mix, norm_pre_ffn, w_gate, w_up, w_down,
             norm_post_ffn):
    T = _host_tables()
    f = lambda a: np.ascontiguousarray(np.asarray(a, dtype=np.float32))
    x = f(x)
    meta = f(meta_tokens)
    shared = {
        "w_in": f(w_in[0]), "w_q_b": f(w_q_b[0]), "w_kv_b": f(w_kv_b[0]), "w_fourier": f(w_fourier[0]),
        "w_o": f(w_o[0]), "w_gate": f(w_gate[0]), "w_up": f(w_up[0]), "w_down": f(w_down[0]),
        "g_pre": f(np.asarray(norm_pre_mix[0]).reshape(8, 128).T),
        "g_q": f(np.asarray(q_a_norm[0]).reshape(2, 128).T),
        "g_kv": f(np.asarray(kv_a_norm[0]).reshape(128, 1)),
        "g_a": f(np.asarray(mix_gain_attn[0]).reshape(4, 128).T),
        "g_f": f(np.asarray(mix_gain_fourier[0]).reshape(4, 128).T),
        "b_f": f(np.asarray(b_fourier[0]).reshape(4, 128).T),
        "g_post": f(norm_post_mix[0]), "g_preffn": f(norm_pre_ffn[0]), "g_postffn": f(norm_post_ffn[0]),
        "cc": T["cc"], "sc": T["sc"], "ident": T["ident"], "sel": T["sel"],
    }
    maps = []
    for core in range(8):
        b, hf = core // 2, core % 2
        t = T[hf]
        own = x[b, t["own"] - NMETA]
        other = x[b, t["other"] - NMETA]
        xperm = np.concatenate([own, other, meta], axis=0)
        m = dict(shared)
        m.update({"xperm": np.ascontiguousarray(xperm), "ktab": t["ktab"], "qtab": t["qtab"], "dft": t["dft"]})
        maps.append(m)
    return maps


def kernel(**inputs):
    if "nc" not in _CACHE:
        _CACHE["nc"] = build_program()
    nc = _CACHE["nc"]
    maps = _in_maps(**inputs)
    res = run_bass_kernel_spmd(nc, maps, core_ids=list(range(8)))
    out = np.empty((4, SEQ, D), np.float32)
    for core in range(8):
        b, hf = core // 2, core % 2
        out[b, hf * NOWN:(hf + 1) * NOWN] = np.asarray(res.results[core]["y"], dtype=np.float32)
    return out
```
